# Optimizing a Trainium2 kernel written in Bass

```python
import math
import jax, jax.numpy as jnp
from jax import lax
import numpy as np

D_MODEL = 4096
BATCH = 4
SEQ = 2048
DEPTH = 1
DEC_BATCH = 128
DEC_SEQ = 8
PAST_LEN = 8192
PAGE_SIZE = 128

M_HEADS = 8
M_WIDTH = D_MODEL // 2
M_HEAD_DIM = M_WIDTH // M_HEADS
M_CHUNK = 64
A_HEAD_DIM = 64
A_WIDTH = D_MODEL // 2
A_HEADS = A_WIDTH // A_HEAD_DIM
A_KV_HEADS = A_HEADS // 4
A_GROUP = A_HEADS // A_KV_HEADS
A_KV_WIDTH = A_KV_HEADS * A_HEAD_DIM
WINDOW = 128
ROPE_THETA = 10000.0
LN_EPS = 1e-5
DEEPNORM_ALPHA = (2.0 * DEPTH) ** 0.25
DEEPNORM_BETA = (8.0 * DEPTH) ** -0.25

IN_COLUMNS = (('q_m', M_WIDTH), ('k_m', M_WIDTH), ('v_m', M_WIDTH), ('o_m', M_WIDTH), ('z_m', M_WIDTH),
              ('i_m', M_HEADS), ('f_m', M_HEADS),
              ('q_a', A_WIDTH), ('k_a', A_KV_WIDTH), ('v_a', A_KV_WIDTH), ('z_a', A_WIDTH),
              ('g_m', D_MODEL), ('g_a', D_MODEL))
IN_WIDTH = sum(size for _, size in IN_COLUMNS)
SPLIT_POINTS = tuple(int(p) for p in np.cumsum([size for _, size in IN_COLUMNS])[:-1])

kernel_name = 'hybrid_mlstm_swa_sink_decode_step'


def _layer_norm(x, g, b):
    xf = x.astype(jnp.float32)
    mu = xf.mean(-1, keepdims=True)
    var = jnp.mean(jnp.square(xf - mu), -1, keepdims=True)
    return ((xf - mu) * lax.rsqrt(var + LN_EPS) * g + b).astype(x.dtype)


def _rope(x, pos):
    half = x.shape[-1] // 2
    inv = ROPE_THETA ** (-jnp.arange(half, dtype=jnp.float32) / half)
    ang = pos.astype(jnp.float32)[:, None] * inv[None, :]
    cos = jnp.cos(ang)[None, :, None, :]
    sin = jnp.sin(ang)[None, :, None, :]
    xf = x.astype(jnp.float32)
    x1, x2 = xf[..., :half], xf[..., half:]
    return jnp.concatenate([x1 * cos - x2 * sin, x2 * cos + x1 * sin], -1).astype(x.dtype)


def _mlstm_scan(q, k, v, ig, lf, C0, n0, m0):
    B, H, T, _ = q.shape
    L = math.gcd(T, M_CHUNK)
    nc = T // L
    causal = jnp.tril(jnp.ones((L, L), bool))

    def to_chunks(a):
        return jnp.moveaxis(a.reshape((B, H, nc, L) + a.shape[3:]), 2, 0)

    def step(carry, xs):
        C, n, m = carry
        qc, kc, vc, ic, fc = xs
        b = jnp.cumsum(fc, axis=-1)
        log_d = b[..., :, None] - b[..., None, :] + ic[..., None, :]
        log_d = jnp.where(causal, log_d, -jnp.inf)
        a = b + m[..., None]
        m_t = jnp.maximum(a, log_d.max(-1))
        s = jnp.einsum('bhtd,bhsd->bhts', qc, kc) * jnp.exp(log_d - m_t[..., None])
        inter = jnp.exp(a - m_t)
        num = jnp.einsum('bhts,bhse->bhte', s, vc) + inter[..., None] * jnp.einsum('bhtd,bhde->bhte', qc, C)
        den = s.sum(-1) + inter * jnp.einsum('bhtd,bhd->bht', qc, n)
        h = num / jnp.maximum(jnp.abs(den), jnp.exp(-m_t))[..., None]
        m_new = m_t[..., -1]
        w = jnp.exp(b[..., -1:] - b + ic - m_new[..., None])
        decay = jnp.exp(b[..., -1] + m - m_new)
        C_new = decay[..., None, None] * C + jnp.einsum('bhsd,bhse->bhde', w[..., None] * kc, vc)
        n_new = decay[..., None] * n + jnp.einsum('bhs,bhsd->bhd', w, kc)
        return (C_new, n_new, m_new), h

    init = (C0.astype(jnp.float32), n0.astype(jnp.float32), m0.astype(jnp.float32))
    (C, n, m), h = lax.scan(step, init, tuple(to_chunks(a) for a in (q, k, v, ig, lf)))
    h = jnp.moveaxis(h, 0, 2).reshape(B, H, T, v.shape[-1])
    return h, C, n, m


def _mlstm_branch(qm, km, vm, om, zm, im, fm, b_if, norm_m_g, state0):
    B, T, _ = qm.shape

    def heads(a):
        return a.reshape(B, T, M_HEADS, -1).transpose(0, 2, 1, 3).astype(jnp.float32)

    q = heads(qm)
    k = heads(km) * (M_HEAD_DIM ** -0.5)
    v = heads(vm)
    ig = (im.astype(jnp.float32) + b_if[:M_HEADS]).transpose(0, 2, 1)
    lf = jax.nn.log_sigmoid(fm.astype(jnp.float32) + b_if[M_HEADS:]).transpose(0, 2, 1)
    h, C, n, m = _mlstm_scan(q, k, v, ig, lf, *state0)
    h = jax.nn.sigmoid(om.astype(jnp.float32)).reshape(B, T, M_HEADS, M_HEAD_DIM) * h.transpose(0, 2, 1, 3)
    mu = h.mean(-1, keepdims=True)
    var = jnp.mean(jnp.square(h - mu), -1, keepdims=True)
    h = (h - mu) * lax.rsqrt(var + LN_EPS) * norm_m_g.reshape(M_HEADS, M_HEAD_DIM)
    out = h.reshape(B, T, M_WIDTH).astype(qm.dtype) * jax.nn.silu(zm)
    return out, C, n, m


def _sink_probs(s, mask, sink):
    s = jnp.where(mask, s, -jnp.inf)
    mx = jnp.maximum(s.max(-1, keepdims=True), sink)
    p = jnp.exp(s - mx)
    return p / (p.sum(-1, keepdims=True) + jnp.exp(sink - mx))


def _swa_prompt(q, k, v, sinks):
    B, S, H, hd = q.shape
    nb = S // WINDOW
    qb = q.reshape(B, nb, WINDOW, A_KV_HEADS, A_GROUP, hd)
    kb = k.reshape(B, nb, WINDOW, A_KV_HEADS, hd)
    vb = v.reshape(B, nb, WINDOW, A_KV_HEADS, hd)

    def with_prev(a):
        prev = jnp.pad(a[:, :-1], ((0, 0), (1, 0), (0, 0), (0, 0), (0, 0)))
        return jnp.concatenate([prev, a], axis=2)

    kk, vv = with_prev(kb), with_prev(vb)
    s = jnp.einsum('bnqkgd,bnskd->bnkgqs', qb, kk).astype(jnp.float32) * (hd ** -0.5)
    blk = jnp.arange(nb)[:, None, None] * WINDOW
    qpos = blk + jnp.arange(WINDOW)[None, :, None]
    kpos = blk - WINDOW + jnp.arange(2 * WINDOW)[None, None, :]
    rel = qpos - kpos
    mask = (rel >= 0) & (rel < WINDOW) & (kpos >= 0)
    sink = sinks.reshape(A_KV_HEADS, A_GROUP)[:, :, None, None].astype(jnp.float32)
    p = _sink_probs(s, mask[None, :, None, None], sink)
    o = jnp.einsum('bnkgqs,bnskd->bnqkgd', p.astype(vv.dtype), vv)
    return o.reshape(B, S, H * hd)


def _swa_sample(q, k, v, cache_k, cache_v, sinks):
    B, T, H, hd = q.shape
    w_buf = cache_k.shape[1]
    kk = jnp.concatenate([cache_k.astype(k.dtype), k], axis=1)
    vv = jnp.concatenate([cache_v.astype(v.dtype), v], axis=1)
    qg = q.reshape(B, T, A_KV_HEADS, A_GROUP, hd)
    s = jnp.einsum('btkgd,bskd->bkgts', qg, kk).astype(jnp.float32) * (hd ** -0.5)
    qpos = PAST_LEN + jnp.arange(T)
    kpos = PAST_LEN - w_buf + jnp.arange(w_buf + T)
    rel = qpos[:, None] - kpos[None, :]
    mask = (rel >= 0) & (rel < WINDOW)
    sink = sinks.reshape(A_KV_HEADS, A_GROUP)[:, :, None, None].astype(jnp.float32)
    p = _sink_probs(s, mask, sink)
    o = jnp.einsum('bkgts,bskd->btkgd', p.astype(vv.dtype), vv).reshape(B, T, H * hd)
    return o, kk[:, -w_buf:], vv[:, -w_buf:]


def _hybrid_layer(x, positions, state0, attend, w_in, b_if, norm_m_g, w_bm, w_ba, w_out, ln_g, ln_b):
    B, T, _ = x.shape
    qm, km, vm, om, zm, im, fm, qa, ka, va, za, gm, ga = jnp.split(x @ w_in, SPLIT_POINTS, axis=-1)
    mix_m, C, n, m = _mlstm_branch(qm, km, vm, om, zm, im, fm, b_if, norm_m_g, state0)
    qa = _rope(qa.reshape(B, T, A_HEADS, A_HEAD_DIM), positions)
    ka = _rope(ka.reshape(B, T, A_KV_HEADS, A_HEAD_DIM), positions)
    va = va.reshape(B, T, A_KV_HEADS, A_HEAD_DIM)
    att, k_buf, v_buf = attend(qa, ka, va)
    mix_a = att * jax.nn.silu(za)
    merged = jax.nn.sigmoid(gm) * (mix_m @ w_bm) + jax.nn.sigmoid(ga) * (mix_a @ w_ba)
    y = _layer_norm(DEEPNORM_ALPHA * x + merged @ w_out, ln_g, ln_b)
    return y, C, n, m, k_buf, v_buf


def setup_inputs(seed: int = 0) -> dict:
    key = jax.random.key(seed)
    ks = jax.random.split(key, 18)
    f32 = jnp.float32
    w_buf = min(WINDOW, PAST_LEN)

    def nrm(k, shape):
        return jax.random.normal(k, shape, f32)

    col_scale = jnp.concatenate([jnp.full((size,), DEEPNORM_BETA if name in ('v_m', 'v_a') else 1.0, f32)
                                 for name, size in IN_COLUMNS])
    return {
        'x_prompt': nrm(ks[0], (BATCH, SEQ, D_MODEL)),
        'x_sample': nrm(ks[1], (DEC_BATCH, DEC_SEQ, D_MODEL)),
        'state_C': 0.3 * nrm(ks[2], (DEPTH, DEC_BATCH, M_HEADS, M_HEAD_DIM, M_HEAD_DIM)),
        'state_n': 0.1 * nrm(ks[3], (DEPTH, DEC_BATCH, M_HEADS, M_HEAD_DIM)),
        'state_m': 0.5 * nrm(ks[4], (DEPTH, DEC_BATCH, M_HEADS)),
        'cache_k': nrm(ks[5], (DEPTH, DEC_BATCH, w_buf, A_KV_HEADS, A_HEAD_DIM)),
        'cache_v': nrm(ks[6], (DEPTH, DEC_BATCH, w_buf, A_KV_HEADS, A_HEAD_DIM)),
        'w_in': nrm(ks[7], (DEPTH, D_MODEL, IN_WIDTH)) * (D_MODEL ** -0.5) * col_scale,
        'b_if': jnp.concatenate([0.1 * nrm(ks[8], (DEPTH, M_HEADS)),
                                 3.0 + 3.0 * jax.random.uniform(ks[9], (DEPTH, M_HEADS), f32)], axis=-1),
        'norm_m_g': 1.0 + 0.02 * nrm(ks[10], (DEPTH, M_WIDTH)),
        'attn_sinks': 0.5 * nrm(ks[11], (DEPTH, A_HEADS)),
        'w_bm': nrm(ks[12], (DEPTH, M_WIDTH, D_MODEL)) * (M_WIDTH ** -0.5 * DEEPNORM_BETA),
        'w_ba': nrm(ks[13], (DEPTH, A_WIDTH, D_MODEL)) * (A_WIDTH ** -0.5 * DEEPNORM_BETA),
        'w_out': nrm(ks[14], (DEPTH, D_MODEL, D_MODEL)) * (D_MODEL ** -0.5 * DEEPNORM_BETA),
        'ln_g': 1.0 + 0.02 * nrm(ks[15], (DEPTH, D_MODEL)),
        'ln_b': 0.02 * nrm(ks[16], (DEPTH, D_MODEL)),
    }


def reference(x_prompt, x_sample, state_C, state_n, state_m, cache_k, cache_v, w_in, b_if, norm_m_g,
              attn_sinks, w_bm, w_ba, w_out, ln_g, ln_b):
    bp, seq = x_prompt.shape[0], x_prompt.shape[1]
    w_buf = cache_k.shape[2]
    pos_prompt = jnp.arange(seq)
    pos_sample = PAST_LEN + jnp.arange(x_sample.shape[1])
    fresh_state = (jnp.zeros((bp, M_HEADS, M_HEAD_DIM, M_HEAD_DIM), jnp.float32),
                   jnp.zeros((bp, M_HEADS, M_HEAD_DIM), jnp.float32),
                   jnp.zeros((bp, M_HEADS), jnp.float32))
    y_p, y_s = x_prompt, x_sample
    outs_p, outs_s = [], []
    for l in range(DEPTH):
        weights = (w_in[l], b_if[l], norm_m_g[l], w_bm[l], w_ba[l], w_out[l], ln_g[l], ln_b[l])
        sinks = attn_sinks[l]
        ck, cv = cache_k[l], cache_v[l]

        def attend_prompt(q, k, v):
            return _swa_prompt(q, k, v, sinks), k[:, seq - w_buf:], v[:, seq - w_buf:]

        def attend_sample(q, k, v):
            return _swa_sample(q, k, v, ck, cv, sinks)

        y_p, *st_p = _hybrid_layer(y_p, pos_prompt, fresh_state, attend_prompt, *weights)
        y_s, *st_s = _hybrid_layer(y_s, pos_sample, (state_C[l], state_n[l], state_m[l]), attend_sample, *weights)
        outs_p.append(st_p)
        outs_s.append(st_s)

    def stack(outs, i, like):
        return jnp.stack([o[i] for o in outs]).astype(like.dtype)

    return (y_p, y_s,
            stack(outs_p, 0, state_C), stack(outs_p, 1, state_n), stack(outs_p, 2, state_m),
            stack(outs_p, 3, cache_k), stack(outs_p, 4, cache_v),
            stack(outs_s, 0, state_C), stack(outs_s, 1, state_n), stack(outs_s, 2, state_m),
            stack(outs_s, 3, cache_k), stack(outs_s, 4, cache_v))
```

```python
import math
from contextlib import ExitStack

import numpy as np
import ml_dtypes
import concourse.bass as bass
import concourse.mybir as mybir
from concourse.bass_utils import run_bass_kernel_spmd

F32 = mybir.dt.float32
BF16 = mybir.dt.bfloat16
AF = mybir.ActivationFunctionType
ALU = mybir.AluOpType
AX = mybir.AxisListType

EPOCH = 16000
NEG = -30000.0
LN_EPS = 1e-5
ROPE_THETA = 10000.0
PAST_LEN = 8192


class Buf:
    __slots__ = ("name", "w", "r", "dsem", "dcnt", "psum")

    def __init__(self, name, psum=False):
        self.name = name
        self.psum = psum
        self.w = None
        self.r = {}
        self.dsem = None
        self.dcnt = 0


class Emitter:
    ENGS = ("pe", "dve", "act", "pool", "sp")

    def __init__(self, nc, stack):
        self.nc = nc
        self.stack = stack
        self.streams = {e: [] for e in self.ENGS}
        self.sem = {}
        self.cnt = {}
        self.semid = 0
        for e in self.ENGS:
            self.sem[e] = self._newsem(e)
            self.cnt[e] = 0
        self.pending = {e: False for e in self.ENGS}
        self.known = {e: {} for e in self.ENGS}
        self.dma_bufs = []
        self.tags = None

    def _newsem(self, tag):
        self.semid += 1
        return self.stack.enter_context(self.nc.semaphore(f"s{self.semid}_{tag}"))

    def _wait(self, eng, tok):
        if tok is None:
            return
        sem, val = tok
        k = self.known[eng]
        key = sem.name
        if k.get(key, 0) >= val:
            return
        if sem is self.sem[eng] and val > self.cnt[eng]:
            return
        k[key] = val
        self.streams[eng].append(("w", sem, val))

    def _deps(self, eng, reads, writes):
        for b in reads:
            self._wait(eng, b.w)
            if b.psum:
                own = self.sem[eng].name
                for k, t in list(b.r.items()):
                    if k != own:
                        self._wait(eng, t)
        for b in writes:
            self._wait(eng, b.w)
            for t in list(b.r.values()):
                self._wait(eng, t)

    def _mark(self, tok, reads, writes):
        for b in writes:
            b.w = tok
            b.r = {}
        for b in reads:
            if b not in writes:
                b.r[tok[0].name] = tok

    def op(self, eng, fn, reads=(), writes=(), sig=True):
        if self.tags is not None:
            import sys as _s
            fr = _s._getframe(2)
            self.tags[eng].append((fr.f_lineno, fr.f_back.f_lineno if fr.f_back else -1))
        self._deps(eng, reads, writes)
        if sig and self.cnt[eng] >= EPOCH and not self.pending[eng]:
            self.sem[eng] = self._newsem(eng)
            self.cnt[eng] = 0
        if sig:
            self.cnt[eng] += 1
            tok = (self.sem[eng], self.cnt[eng])
            self.streams[eng].append(("o", fn, self.sem[eng]))
            self.pending[eng] = False
        else:
            tok = (self.sem[eng], self.cnt[eng] + 1)
            self.streams[eng].append(("o", fn, None))
            self.pending[eng] = True
        self._mark(tok, reads, writes)
        return tok

    def dma(self, eng, out_ap, in_ap, reads=(), writes=(), **kw):
        self._deps(eng, reads, writes)
        main = (list(writes) + list(reads))[0]
        if main.dsem is None:
            main.dsem = self._newsem("d" + main.name)
            self.dma_bufs.append(main)
        main.dcnt += 16
        tok = (main.dsem, main.dcnt)
        self.streams[eng].append(("d", out_ap, in_ap, kw, main.dsem))
        self._mark(tok, reads, writes)
        return tok

    def finish(self):
        for b in self.dma_bufs:
            self._wait("sp", (b.dsem, b.dcnt))

    def replay(self):
        nc = self.nc
        streams = self.streams

        def run(h, items):
            for it in items:
                if it[0] == "w":
                    h.wait_ge(it[1], it[2])
                elif it[0] == "o":
                    ins = it[1](h)
                    if it[2] is not None:
                        ins.then_inc(it[2], 1)
                else:
                    h.dma_start(out=it[1], in_=it[2], **it[3]).then_inc(it[4], 16)

        with nc.Block() as block:
            @block.tensor
            def _(e):
                run(e, streams["pe"])

            @block.vector
            def _(e):
                run(e, streams["dve"])

            @block.scalar
            def _(e):
                run(e, streams["act"])

            @block.gpsimd
            def _(e):
                run(e, streams["pool"])

            @block.sync
            def _(e):
                run(e, streams["sp"])


def make_cfg(D=4096, NPT=8, GT=3):
    NH = D // 512
    MW = D // 2
    c = dict(D=D, NPT=NPT, GT=GT, NH=NH, MW=MW, KC=D // 128)
    o = 0
    for name, size in (("qm", MW), ("km", MW), ("vm", MW), ("om", MW), ("zm", MW), ("i", NH), ("f", NH),
                       ("qa", MW), ("ka", NH * 64), ("va", NH * 64), ("za", MW), ("gm", D), ("ga", D)):
        c["c_" + name] = o
        o += size
    c["INW"] = o
    c["alpha"] = 2.0 ** 0.25
    return c


def build(cfg):
    D, NPT, GT, NH, MW, KC = cfg["D"], cfg["NPT"], cfg["GT"], cfg["NH"], cfg["MW"], cfg["KC"]
    INW = cfg["INW"]
    NT = NPT + 1
    KCM = MW // 128
    SLK = min(4, KC)
    TG = GT * 128
    alpha = cfg["alpha"]

    nc = bass.Bass("TRN2", target_bir_lowering=False)

    def din(name, shape, dt=F32):
        return nc.dram_tensor(name, list(shape), dt, kind="ExternalInput").ap()

    def dout(name, shape, dt=F32):
        return nc.dram_tensor(name, list(shape), dt, kind="ExternalOutput").ap()

    x_d = din("x", [NT, 128, D])
    xp_d = din("xp", [NPT, 128, D])
    WT_TOTAL = INW * KC + 2 * D * KCM + D * KC
    wt_d = din("wt", [128, WT_TOTAL])
    pieces_reg = cfg.setdefault("_pieces", {})
    pieces_tot = [0]

    def wpiece(mat, c0, n, nkc):
        key = (mat, c0, n, nkc)
        if key not in pieces_reg:
            pieces_reg[key] = pieces_tot[0]
            pieces_tot[0] += n * nkc
            assert pieces_tot[0] <= WT_TOTAL
        return ("wt", pieces_reg[key])
    bif_d = din("bif", [128, 2 * NH])
    sinks_d = din("sinks", [128, 4 * NH])
    ng_d = din("ng", [128, MW])
    lng_d = din("lng", [128, D])
    lnb_d = din("lnb", [128, D])
    sC_d = din("sC", [16, NH, 256, 256])
    sn_d = din("sn", [16, NH, 256])
    sm_d = din("sm", [16, NH])
    ck_d = din("ck", [16, 128, NH * 64])
    cv_d = din("cv", [16, 128, NH * 64])
    ident_d = din("ident", [128, 128])
    amask_d = din("amask", [128, 3, 256])
    dmask_d = din("dmask", [128, 2, 128])
    rope_d = din("rope", [128, NT + 1, 2, 64])
    seqm_d = din("seqm", [128, 16])
    colm_d = din("colm", [128, 16 * 128], BF16)
    flag_d = din("flag", [128, 1])
    rst_d = din("rst", [NH, 2, 128])

    y_d = dout("y", [NT, 128, D])
    pC_d = dout("pC", [NH, 256, 256])
    pn_d = dout("pn", [NH, 256])
    pm_d = dout("pm", [NH, 1])
    pk_d = dout("pk", [128, NH * 64])
    pv_d = dout("pv", [128, NH * 64])
    oC_d = dout("oC", [16, NH, 256, 256])
    on_d = dout("on", [16, NH, 256])
    om_d = dout("om", [16, NH])
    ok_d = dout("ok", [16, 128, NH * 64])
    ov_d = dout("ov", [16, 128, NH * 64])

    dbg_d = dout("dbg", [128, KC * TG], BF16) if cfg.get("debug") else None
    st = ExitStack()
    em = Emitter(nc, st)
    bufs = {}

    def B(name, psum=False):
        if name not in bufs:
            bufs[name] = Buf(name, psum)
        return bufs[name]

    def SB(name, shape, dt=F32):
        return st.enter_context(nc.sbuf_tensor("sb_" + name, list(shape), dt))

    def PS(name, shape, dt=F32):
        return st.enter_context(nc.psum_tensor("ps_" + name, list(shape), dt))

    regA = SB("regA", [128, GT * D])
    regA_b = regA[:].bitcast(BF16)
    xT = regA_b[:, 0:KC * TG].rearrange("p (c t) -> p c t", c=KC)
    mixT = regA_b[:, KC * TG:2 * KC * TG].rearrange("p (c t) -> p c t", c=KC)
    ypre = regA[:].rearrange("p (j d) -> p j d", j=GT)
    wbf = [SB(f"wbf{i}", [128, KC, 256], BF16) for i in range(2)]
    misc = SB("misc", [128, 2048])
    cst = SB("cst", [128, NH, 2, 257])
    cbf = [SB(f"cbf{i}", [128, 2, 257], BF16) for i in range(2)]
    PH = 2 * TG + 2 * TG + GT * 256 + GT * 257 + GT * 256 + GT * 256 + 4 * TG + TG + GT * 64 + GT * 256
    PHT = max(2 * PH, KC * TG)
    phreg = SB("phreg", [128, PHT], BF16)
    mergedT = phreg[:, 0:KC * TG].rearrange("p (c t) -> p c t", c=KC)

    def ph_views(par):
        o = par * PH
        v = {}

        def take(name, n, pat=None, **kw):
            nonlocal o
            a = phreg[:, o:o + n]
            o += n
            v[name] = a.rearrange(pat, **kw) if pat else a

        take("qT", 2 * TG, "p (c t) -> p c t", c=2)
        take("kT", 2 * TG, "p (c t) -> p c t", c=2)
        take("km", GT * 256, "p (j e) -> p j e", j=GT)
        take("vaug", GT * 257, "p (j e) -> p j e", j=GT)
        take("sgo", GT * 256, "p (j e) -> p j e", j=GT)
        take("gz", GT * 256, "p (j e) -> p j e", j=GT)
        take("qaT", 4 * TG, "p (g t) -> p g t", g=4)
        take("kaT", TG, "p (j t) -> p j t", j=GT)
        take("va", GT * 64, "p (j e) -> p j e", j=GT)
        take("za", GT * 256, "p (j e) -> p j e", j=GT)
        return v

    phv = [ph_views(0), ph_views(1)]
    kaT_c = SB("kaT_c", [128, NH, 128], BF16)
    va_c = SB("va_c", [128, NH, 64], BF16)
    ident_f = SB("ident_f", [128, 128])
    ident_b = SB("ident_b", [128, 128], BF16)
    amask = SB("amask", [128, 3, 256])
    dmask = SB("dmask", [128, 2, 128])
    rope = SB("rope", [128, GT, 2, 64])
    seqm = SB("seqm", [128, 16])
    colm = SB("colm", [128, 16, 128], BF16)
    flag = SB("flag", [128, 1])
    rst = SB("rst", [NH, 2, 128])
    bif = SB("bif", [128, 2 * NH])
    sinks = SB("sinks", [128, 4 * NH])
    ngh = [SB(f"ngh{i}", [128, 256]) for i in range(2)]
    ones_nh = SB("ones_nh", [NH, 128])
    ones_bf = SB("ones_bf", [128, 1], BF16)
    gl = SB("gl", [128, GT, 2 * NH])
    gA = SB("gA", [NH, TG])
    gBm = SB("gBm", [NH, TG])
    gC = SB("gC", [NH, TG])
    gD = SB("gD", [NH, TG])
    gW = SB("gW", [NH, TG])
    gsm = SB("gsm", [NH, 64])
    bcsrc = SB("bcsrc", [NH, 160])
    gtm = SB("gtm", [128, GT, 4, NH])
    DTp = SB("DTp", [128, 128])
    DTe = SB("DTe", [128, 128])
    SDT = SB("SDT", [128, 128], BF16)
    P1s = SB("P1s", [128, 257])
    ndt = SB("ndt", [128, 257])
    hot = SB("hot", [128, 256])
    tmpm = SB("tmpm", [128, 256])
    kw = SB("kw", [128, 256], BF16)
    mixtm = SB("mixtm", [128, 256], BF16)
    sml = SB("sml", [128, 32])
    bcs = SB("bcs", [128, 32])
    stats = SB("stats", [128, 8, 6])
    sc = SB("sc", [128, 256])
    pbf = SB("pbf", [128, 256], BF16)
    pT = SB("pT", [128, 2, 128], BF16)
    mixa = SB("mixa", [128, 256], BF16)
    asml = SB("asml", [128, 16])
    ztmp = SB("ztmp", [128, 256])
    cstg = SB("cstg", [128, 384])
    smo = SB("smo", [NH, 16])
    ropet = SB("ropet", [128, 512])
    qrot = SB("qrot", [128, 256], BF16)
    kvf = SB("kvf", [128, 128])
    krot = SB("krot", [128, 64])
    krb = SB("krb", [128, 64], BF16)
    evb = SB("evb", [128, 256], BF16)
    mskb = SB("mskb", [128, 16, 128], BF16)
    mqf = SB("mqf", [128, 2, 128])
    wm16 = SB("wm16", [128, 16])
    nall = SB("nall", [128, 2, 16])
    nout = SB("nout", [128, 2, 16])
    kw2 = SB("kw2", [128, 2, 128], BF16)
    KcT = SB("KcT", [128, 16, 128], BF16)
    Vc = SB("Vc", [128, 16, 64], BF16)
    NCS = 6
    csf = [SB(f"csf{i}", [128, 257]) for i in range(NCS)]

    sgA = SB("sgA", [NH, 16])
    sgB = SB("sgB", [NH, 16])

    pbank = [PS(f"pb{i}", [128, 512]) for i in range(7)]
    pbf16 = PS("pb7", [128, 1024], BF16)
    PBK = [B(f"pb{i}", psum=True) for i in range(8)]

    def wb(slot, kc):
        return B(f"wbf{slot}_{kc // SLK}")

    def mm(out, lhsT, rhs, start, stop, reads, writes, sig=True):
        em.op("pe", lambda e: e.matmul(out, lhsT=lhsT, rhs=rhs, start=start, stop=stop), reads, writes, sig)

    def tr(out, in_, ident, reads, writes, sig=True):
        em.op("pe", lambda e: e.transpose(out=out, in_=in_, identity=ident), reads, writes, sig)

    def act(out, in_, func, reads, writes, bias=None, scale=None, accum=None):
        kw_ = {}
        if bias is not None:
            kw_["bias"] = bias
        if scale is not None:
            kw_["scale"] = scale
        if accum is not None:
            kw_["accum_out"] = accum
        em.op("act", lambda e: e.activation(out=out, in_=in_, func=func, **kw_), reads, writes)

    def tt(eng, out, a, b, op, reads, writes):
        em.op(eng, lambda e: e.tensor_tensor(out=out, in0=a, in1=b, op=op), reads, writes)

    def ts(eng, out, a, s1, op0, reads, writes, s2=None, op1=None):
        if op1 is None:
            em.op(eng, lambda e: e.tensor_scalar(out=out, in0=a, scalar1=s1, scalar2=None, op0=op0), reads, writes)
        else:
            em.op(eng, lambda e: e.tensor_scalar(out=out, in0=a, scalar1=s1, scalar2=s2, op0=op0, op1=op1),
                  reads, writes)

    def stt(out, in0, scalar, in1, op0, op1, reads, writes):
        em.op("dve", lambda e: e.scalar_tensor_tensor(out=out, in0=in0, scalar=scalar, in1=in1, op0=op0, op1=op1),
              reads, writes)

    def cp(eng, out, in_, reads, writes):
        if eng == "act":
            em.op("act", lambda e: e.copy(out=out, in_=in_), reads, writes)
        else:
            em.op(eng, lambda e: e.tensor_copy(out=out, in_=in_), reads, writes)

    def E(eng, method, reads, writes, *args, **kwargs):
        em.op(eng, lambda e: getattr(e, method)(*args, **kwargs), reads, writes)

    def dma(out, in_, reads=(), writes=(), eng="sp", **kw_):
        em.dma(eng, out, in_, reads, writes, **kw_)

    for t, d, nm in ((ident_f, ident_d, "ident_f"), (amask, amask_d, "amask"), (dmask, dmask_d, "dmask"),
                     (seqm, seqm_d, "seqm"), (flag, flag_d, "flag"), (rst, rst_d, "rst"),
                     (bif, bif_d, "bif"), (sinks, sinks_d, "sinks")):
        dma(t[:], d, writes=[B(nm)])
    dma(colm[:].rearrange("p s t -> p (s t)"), colm_d, writes=[B("colm")])
    cp("dve", ident_b[:], ident_f[:], [B("ident_f")], [B("ident_b")])
    em.op("dve", lambda e: e.memset(ones_nh[:], 1.0), [], [B("ones_nh")])
    em.op("dve", lambda e: e.memset(ones_bf[:], 1.0), [], [B("ones_bf")])
    em.op("dve", lambda e: e.memset(cst[:], 0.0), [], [B("cst")])
    em.op("dve", lambda e: e.memset(gsm[:], 0.0), [], [B("gsm")])
    CONST = [B("ident_f"), B("ident_b")]

    wstate = {"slab": 0, "blk": 0}

    class Task:
        def __init__(self, pieces, run):
            self.pieces = pieces
            self.run = run
            self.slabs = []
            self.wslot = None
            for pi, (ap, nkc, ncols, coff) in enumerate(pieces):
                for k0 in range(0, nkc, SLK):
                    self.slabs.append((ap, k0, min(SLK, nkc - k0), ncols, coff))
            self.next = 0

        def load_one(self):
            if self.next >= len(self.slabs):
                return False
            if self.wslot is None:
                self.wslot = wstate["blk"] % 2
                wstate["blk"] += 1
            ap, k0, nk, ncols, coff = self.slabs[self.next]
            self.next += 1
            off = ap[1]
            src = wt_d[:, off + k0 * ncols:off + (k0 + nk) * ncols].rearrange("p (k n) -> p k n", k=nk)
            dma(wbf[self.wslot][:, k0:k0 + nk, coff:coff + ncols], src, writes=[wb(self.wslot, k0)], eng="pool")
            return True

        def load_all(self):
            while self.load_one():
                pass

    tasks = []
    gens = []

    rr = [0]

    def advance(n=1):
        for _ in range(n):
            if not gens:
                return
            rr[0] = (rr[0] + 1) % len(gens)
            g = gens[rr[0]]
            try:
                next(g)
            except StopIteration:
                gens.remove(g)

    def drain():
        while gens:
            advance()

    def load_x_tile(src_tile_ap, j):
        for half in range(D // 2048 if D >= 2048 else 1):
            w = min(2048, D)
            dma(misc[:, 0:w], src_tile_ap[:, half * w:(half + 1) * w],
                writes=[B("misc"), B("msig0"), B("msig1"), B("mprod0"), B("mprod1")])
            for q in range(w // 512):
                bk = 3 + (q % 2)
                for i in range(4):
                    c = (half * w + q * 512) // 128 + i
                    tr(pbank[bk][:, i * 128:(i + 1) * 128], misc[:, q * 512 + i * 128: q * 512 + (i + 1) * 128],
                       ident_f[:], [B("misc"), B("ident_f")], [PBK[bk]], sig=(i == 3))
                c0 = (half * w + q * 512) // 128
                eng = "act" if q % 2 else "dve"
                cp(eng, xT[:, c0:c0 + 4, j * 128:(j + 1) * 128],
                   pbank[bk][:, 0:512].rearrange("p (c t) -> p c t", c=4), [PBK[bk]], [B("xT")])

    pj = {"bank": 0}

    hook = {"nxt": None}

    def mini():
        if hook["nxt"] is not None:
            hook["nxt"].load_one()
        advance(1)

    def proj_tiles(wslot, nkc, ncols, ntiles, evac, tick, lhs=None, lhs_buf=None):
        lhs = xT if lhs is None else lhs
        lhs_buf = [B("xT")] if lhs_buf is None else lhs_buf
        pend = None
        for j in range(ntiles):
            bk = pj["bank"] % 3
            pj["bank"] += 1
            for kc in range(nkc):
                mm(pbank[bk][:, 0:ncols], lhs[:, kc, j * 128:(j + 1) * 128], wbf[wslot][:, kc, 0:ncols],
                   kc == 0, kc == nkc - 1, lhs_buf + [wb(wslot, kc)], [PBK[bk]], sig=(kc == nkc - 1))
                if kc % 8 == 7 and kc != nkc - 1:
                    mini()
                if kc == 7 and pend is not None:
                    evac(pend[0], pbank[pend[1]], PBK[pend[1]])
                    pend = None
            if pend is not None:
                evac(pend[0], pbank[pend[1]], PBK[pend[1]])
            pend = (j, bk)
            tick()
        evac(pend[0], pbank[pend[1]], PBK[pend[1]])

    def gates_math(tiles):
        nt = len(tiles)
        GB = [B("gates")]
        SCR = bcsrc[:, 0:128]
        for j in range(nt):
            tt("dve", gl[:, j, :], gl[:, j, :], bif[:], ALU.add, [B("bif")] + GB, GB)
        zf = gl[:, 0:nt, NH:2 * NH]
        tmpa = sml[:, 0:nt * NH].rearrange("p (j h) -> p j h", j=nt)
        stt(tmpa, zf, -1.0, zf, ALU.mult, ALU.max, GB, [B("sml")])
        act(tmpa, tmpa, AF.Exp, [B("sml")], [B("sml")], scale=-1.0)
        act(tmpa, tmpa, AF.Ln, [B("sml")], [B("sml")], bias=1.0)
        stt(zf, zf, 0.0, tmpa, ALU.min, ALU.subtract, [B("sml")] + GB, GB)
        for j in range(nt):
            tr(pbank[3][0:NH, j * 128:(j + 1) * 128], gl[:, j, 0:NH], ident_f[:], GB + CONST, [PBK[3]], sig=(j == nt - 1))
        for j in range(nt):
            tr(pbank[4][0:NH, j * 128:(j + 1) * 128], gl[:, j, NH:2 * NH], ident_f[:], GB + CONST, [PBK[4]], sig=(j == nt - 1))
        cp("dve", gA[:, 0:nt * 128], pbank[3][0:NH, 0:nt * 128], [PBK[3]], GB)
        cp("dve", gBm[:, 0:nt * 128], pbank[4][0:NH, 0:nt * 128], [PBK[4]], GB)
        np_ = sum(1 for k, _ in tiles if k in ("p", "x"))
        has_s = any(k == "s" for k, _ in tiles)
        if np_:
            W = np_ * 128
            E("dve", "memset", [], GB, gW[:, 0:W], 0.0)
            E("dve", "tensor_tensor_scan", GB + [B("gsm")], GB, out=gC[:, 0:W], data0=gBm[:, 0:W], data1=gW[:, 0:W],
              initial=gsm[:, 0:1], op0=ALU.add, op1=ALU.add)
            E("dve", "tensor_tensor_scan", GB + [B("gsm")], GB, out=gD[:, 0:W], data0=gBm[:, 0:W], data1=gA[:, 0:W],
              initial=gsm[:, 1:2], op0=ALU.add, op1=ALU.max)
        if has_s:
            o = np_ * 128
            dma(sgA[:], sm_d.rearrange("b h -> h b"), writes=[B("sgA")], allow_slow_non_contiguous=True)
            lf3 = gBm[:, o:o + 128].rearrange("h (s t) -> h s t", t=8)
            ig3 = gA[:, o:o + 128].rearrange("h (s t) -> h s t", t=8)
            tt("dve", sgB[:], lf3[:, :, 0], sgA[:], ALU.add, GB + [B("sgA")], [B("sgB")])
            E("dve", "tensor_tensor_scan", GB + [B("rst")], GB, out=gC[:, o:o + 128], data0=rst[:, 0, :],
              data1=gBm[:, o:o + 128], initial=0.0, op0=ALU.mult, op1=ALU.add)
            tt("dve", SCR, gBm[:, o:o + 128], rst[:, 1, :], ALU.add, GB + [B("rst")], [B("bcsrc")])
            cp("dve", gW[:, 0:128], gA[:, o:o + 128], GB, GB)
            gw3 = gW[:, 0:128].rearrange("h (s t) -> h s t", t=8)
            tt("dve", gw3[:, :, 0], ig3[:, :, 0], sgB[:], ALU.max, GB + [B("sgB")], GB)
            E("dve", "tensor_tensor_scan", GB + [B("bcsrc")], GB, out=gD[:, o:o + 128], data0=SCR, data1=gW[:, 0:128],
              initial=0.0, op0=ALU.add, op1=ALU.max)
            cp("dve", smo[:], gD[:, o:o + 128].rearrange("h (s t) -> h s t", t=8)[:, :, 7], GB, [B("smo")])
        W = nt * 128
        tt("dve", gA[:, 0:W], gC[:, 0:W], gA[:, 0:W], ALU.subtract, GB, GB)
        if np_:
            Wp = np_ * 128
            tt("dve", gsm[:, 2:3], gsm[:, 0:1], gsm[:, 1:2], ALU.subtract, [B("gsm")], [B("gsm")])
            cp("dve", gsm[:, 4:5], gC[:, Wp - 1:Wp], GB, [B("gsm")])
            cp("dve", gsm[:, 5:6], gD[:, Wp - 1:Wp], GB, [B("gsm")])
        tt("dve", gC[:, 0:W], gC[:, 0:W], gD[:, 0:W], ALU.subtract, GB, GB)
        act(gD[:, 0:W], gD[:, 0:W], AF.Exp, GB, GB, scale=-1.0)
        for j, (kind, _) in enumerate(tiles):
            sl = slice(j * 128, (j + 1) * 128)
            if kind in ("p", "x"):
                c = 8 + 4 * j
                if j == 0:
                    cp("dve", gsm[:, c:c + 1], gsm[:, 2:3], [B("gsm")], [B("gsm")])
                else:
                    cp("dve", gsm[:, c:c + 1], gC[:, j * 128 - 1:j * 128], GB, [B("gsm")])
                cp("dve", gsm[:, c + 1:c + 2], gC[:, (j + 1) * 128 - 1:(j + 1) * 128], GB, [B("gsm")])
                ts("dve", gsm[:, c + 3:c + 4], gsm[:, c:c + 1], -1.0, ALU.mult, [B("gsm")], [B("gsm")])
                act(gBm[:, sl], gC[:, sl], AF.Exp, GB + [B("gsm")], GB, bias=gsm[:, c + 3:c + 4])
                act(gW[:, sl], gA[:, sl], AF.Exp, GB + [B("gsm")], GB, scale=-1.0, bias=gsm[:, c + 1:c + 2])
                act(gsm[:, c + 2:c + 3], gsm[:, c + 1:c + 2], AF.Exp, [B("gsm")], [B("gsm")], bias=gsm[:, c + 3:c + 4])
            else:
                u3 = gC[:, sl].rearrange("h (s t) -> h s t", t=8)
                g3 = gA[:, sl].rearrange("h (s t) -> h s t", t=8)
                s3 = SCR.rearrange("h (s t) -> h s t", t=8)
                tt("dve", s3, u3, sgA[:].unsqueeze(2).to_broadcast([NH, 16, 8]), ALU.add, GB + [B("sgA")], [B("bcsrc")])
                act(gBm[:, sl], SCR, AF.Exp, [B("bcsrc")], GB)
                tt("dve", s3, u3[:, :, 7:8].to_broadcast([NH, 16, 8]), g3, ALU.subtract, GB, [B("bcsrc")])
                act(gW[:, sl], SCR, AF.Exp, [B("bcsrc")], GB)
                tt("dve", sgB[:], u3[:, :, 7], sgA[:], ALU.add, GB + [B("sgA")], [B("sgB")])
                act(sgB[:], sgB[:], AF.Exp, [B("sgB")], [B("sgB")])
        for j in range(nt):
            sl = slice(j * 128, (j + 1) * 128)
            for qi, src in enumerate((gA, gBm, gD, gW)):
                tr(pbank[3][:, (j * 4 + qi) * NH:(j * 4 + qi + 1) * NH], src[:, sl], ident_f[0:NH, 0:NH],
                   GB + CONST, [PBK[3]], sig=(qi == 3 and j == nt - 1))
        cp("dve", gtm[:, 0:nt].rearrange("p j q h -> p (j q h)"), pbank[3][:, 0:nt * 4 * NH], [PBK[3]], [B("gtm")])
        for j in range(nt):
            ts("dve", gtm[:, j, 0, :], gtm[:, j, 0, :], -1.0, ALU.mult, [B("gtm")], [B("gtm")])
        if np_:
            cp("dve", gsm[:, 0:1], gsm[:, 4:5], [B("gsm")], [B("gsm")])
            cp("dve", gsm[:, 1:2], gsm[:, 5:6], [B("gsm")], [B("gsm")])

    def mlstm_chunk(kind, j, h, par, tile_idx):
        v = phv[par]
        PHB = [B(f"ph{par}")]
        GB = [B("gates"), B("gtm"), B("gsm")]
        tok = slice(j * 128, (j + 1) * 128)
        MT = [B("mtmp")]
        nd = 16 if kind == "s" else 1
        cp("dve", bcsrc[:, 0:128], gC[:, tok], GB, [B("bcsrc")])
        if kind == "s":
            cp("dve", bcsrc[:, 128:144], sgB[:], [B("sgB")], [B("bcsrc")])
        else:
            cp("dve", bcsrc[:, 128:129], gsm[:, 8 + 4 * j + 2:8 + 4 * j + 3], GB, [B("bcsrc")])
        ts("dve", bcsrc[:, 0:128 + nd], bcsrc[:, 0:128 + nd], ident_f[0:NH, h:h + 1], ALU.mult,
           [B("bcsrc")] + CONST, [B("bcsrc")])
        if kind in ("p", "x"):
            ts("dve", kw[:], v["km"][:, j, :], gtm[:, j, 3, h:h + 1], ALU.mult, PHB + GB, [B("kw")])
        yield
        mm(pbank[3][:, 0:128 + nd], ones_nh[:], bcsrc[:, 0:128 + nd], True, True, [B("ones_nh"), B("bcsrc")], [PBK[3]])
        cp("dve", bcs[:, 0:nd], pbank[3][:, 128:128 + nd], [PBK[3]], [B("bcs")])
        if kind != "x":
            mi = 1 if kind == "s" else 0
            tt("dve", DTp[:], pbank[3][:, 0:128], dmask[:, mi, :], ALU.add, [PBK[3], B("dmask")], MT)
            act(DTe[:], DTp[:], AF.Exp, MT + GB, MT, bias=gtm[:, j, 0, h:h + 1])
            yield
            for dc in range(2):
                mm(pbank[4][:, 0:128], v["kT"][:, dc, tok], v["qT"][:, dc, tok], dc == 0, dc == 1, PHB, [PBK[4]],
                   sig=(dc == 1))
            tt("dve", SDT[:], pbank[4][:, 0:128], DTe[:], ALU.mult, [PBK[4]] + MT, [B("SDT")])
            yield
            mm(pbank[4][:, 0:257], SDT[:], v["vaug"][:, j, :], True, True, [B("SDT")] + PHB, [PBK[4]])
            if kind == "p":
                cb = cbf[h % 2]
                for dc in range(2):
                    mm(pbank[5][:, 0:257], v["qT"][:, dc, tok], cb[:, dc, :], dc == 0, dc == 1,
                       PHB + [B(f"cbf{h % 2}")], [PBK[5]], sig=(dc == 1))
            cp("act", P1s[:], pbank[4][:, 0:257], [PBK[4]], [B("P1s")])
            yield
        if kind == "s":
            it = 0
            dma(hot[0:16, :], sn_d[:, h, :], writes=[B("hot")])
            for dc in range(2):
                tr(pbank[3][:, 320 + dc * 16:336 + dc * 16], hot[0:16, dc * 128:(dc + 1) * 128], ident_f[0:16, 0:16],
                   [B("hot")] + CONST, [PBK[3]], sig=(dc == 1))
            cp("dve", nall[:].rearrange("p c s -> p (c s)"), pbank[3][:, 320:352], [PBK[3]], [B("nall")])
            ts("dve", wm16[:], seqm[:], gtm[:, j, 3, h:h + 1], ALU.mult, GB + [B("seqm")], [B("wm16")])
            PRE = 3

            def issue_load(k):
                dc_, s_ = divmod(k, 16)
                c_ = k % NCS
                dma(csf[c_][:, 0:256], sC_d[s_, h, dc_ * 128:(dc_ + 1) * 128, :], writes=[B(f"csf{c_}")])

            for k in range(PRE):
                issue_load(k)
            for dc in range(2):
                for s in range(16):
                    sl_ = it % 2
                    c4 = it % NCS
                    if it + PRE < 32:
                        issue_load(it + PRE)
                    it += 1
                    CF = [B(f"csf{c4}")]
                    CN = [B(f"csn{c4}")]
                    cp("dve", csf[c4][:, 256:257], nall[:, dc, s:s + 1], [B("nall")], CN)
                    tt("dve", mqf[:, sl_, :], v["qT"][:, dc, tok], colm[:, s, :], ALU.mult, PHB + [B("colm")], [B(f"mq{sl_}")])
                    ts("dve", kw2[:, sl_, :], v["km"][:, j, dc * 128:(dc + 1) * 128], wm16[:, s:s + 1], ALU.mult,
                       PHB + [B("wm16")], [B(f"kw2{sl_}")])
                    yield
                    mm(pbank[5][:, 0:257], mqf[:, sl_, :], csf[c4][:], dc == 0 and s == 0, dc == 1 and s == 15,
                       [B(f"mq{sl_}")] + CF + CN, [PBK[5]], sig=(dc == 1 and s == 15))
                    mm(pbank[3][:, 0:257], kw2[:, sl_, :], v["vaug"][:, j, :], True, True, [B(f"kw2{sl_}")] + PHB, [PBK[3]])
                    stt(csf[c4][:], csf[c4][:], bcs[:, s:s + 1], pbank[3][:, 0:257], ALU.mult, ALU.add,
                        [PBK[3], B("bcs")], CF + CN)
                    cp("dve", nout[:, dc, s:s + 1], csf[c4][:, 256:257], CN, [B("nout")])
                    dma(oC_d[s, h, dc * 128:(dc + 1) * 128, :], csf[c4][:, 0:256], reads=CF)
            for dc in range(2):
                tr(pbank[3][0:16, 256 * 0 + dc * 128:(dc + 1) * 128], nout[:, dc, :], ident_f[:], [B("nout")] + CONST, [PBK[3]],
                   sig=(dc == 1))
            cp("dve", hot[0:16, :], pbank[3][0:16, 0:256], [PBK[3]], [B("hot")])
            dma(on_d[:, h, :], hot[0:16, :], reads=[B("hot")])
            yield
        if kind != "x":
            stt(ndt[:], pbank[5][:, 0:257], gtm[:, j, 1, h:h + 1], P1s[:], ALU.mult, ALU.add,
                [PBK[5], B("P1s")] + GB, [B("ndt")])
            stt(sml[:, 5:6], ndt[:, 256:257], -1.0, ndt[:, 256:257], ALU.mult, ALU.max, [B("ndt")], [B("sml")])
            ts("dve", sml[:, 0:1], sml[:, 5:6], gtm[:, j, 2, h:h + 1], ALU.max, [B("sml")] + GB, [B("sml")])
            em.op("dve", lambda e: e.reciprocal(out=sml[:, 1:2], in_=sml[:, 0:1]), [B("sml")], [B("sml")])
            stt(hot[:], ndt[:, 0:256], sml[:, 1:2], v["sgo"][:, j, :], ALU.mult, ALU.mult, [B("ndt"), B("sml")] + PHB,
                [B("hot")])
            em.op("dve", lambda e: e.bn_stats(out=stats[:, 0, :], in_=hot[:]), [B("hot")], [B("stats")])
            em.op("dve", lambda e: e.bn_aggr(out=sml[:, 2:4], in_=stats[:, 0, :]), [B("stats")], [B("sml")])
            act(sml[:, 4:5], sml[:, 3:4], AF.Ln, [B("sml")], [B("sml")], bias=LN_EPS)
            act(sml[:, 4:5], sml[:, 4:5], AF.Exp, [B("sml")], [B("sml")], scale=-0.5)
            ts("dve", tmpm[:], v["gz"][:, j, :], sml[:, 4:5], ALU.mult, PHB + [B("sml")], [B("tmpm")])
            stt(mixtm[:], hot[:], sml[:, 2:3], tmpm[:], ALU.subtract, ALU.mult, [B("hot"), B("sml"), B("tmpm")],
                [B("mixtm")])
            yield
            yield
            for dc in range(2):
                tr(pbf16[:, 512 + dc * 128:512 + (dc + 1) * 128], mixtm[:, dc * 128:(dc + 1) * 128], ident_b[:],
                   [B("mixtm")] + CONST, [PBK[7]], sig=(dc == 1))
            cp("act", mixT[:, 2 * h:2 * h + 2, tok], pbf16[:, 512:768].rearrange("p (c t) -> p c t", c=2), [PBK[7]],
               [B("mixT")])
            yield
        if kind in ("p", "x"):
            cub = (5, 3)
            for dc in range(2):
                mm(pbank[cub[dc]][:, 0:257], kw[:, dc * 128:(dc + 1) * 128], v["vaug"][:, j, :], True, True, [B("kw")] + PHB,
                   [PBK[cub[dc]]])
            yield
            for dc in range(2):
                stt(cst[:, h, dc, :], cst[:, h, dc, :], bcs[:, 0:1], pbank[cub[dc]][:, 0:257], ALU.mult, ALU.add,
                    [PBK[cub[dc]], B("bcs")], [B(f"cst{h}")])
            yield
            if kind == "p":
                cp("act", cbf[h % 2][:], cst[:, h, :, :], [B(f"cst{h}")], [B(f"cbf{h % 2}")])
        yield

    def attn_units(kind, j, h, par, first_tile):
        v = phv[par]
        PHB = [B(f"ph{par}")]
        tok = slice(j * 128, (j + 1) * 128)
        AT = [B("atmp")]
        mi = 2 if kind == "s" else (1 if first_tile else 0)
        if kind == "s":
            cstb = cstg[:, 256:384].bitcast(BF16)
            for s2 in range(8):
                dma(cstg[:, 0:128].rearrange("p (s c) -> p s c", s=2),
                    ck_d[2 * s2:2 * s2 + 2, :, h * 64:(h + 1) * 64].rearrange("s k c -> k s c"), writes=[B("cstg")])
                cp("dve", cstb[:, 0:128], cstg[:, 0:128], [B("cstg")], [B("cstgb")])
                for i in range(2):
                    tr(pbf16[0:64, 768 + i * 128:768 + (i + 1) * 128], cstb[:, i * 64:(i + 1) * 64], ident_b[:],
                       [B("cstgb")] + CONST, [PBK[7]], sig=(i == 1))
                cp("act", KcT[0:64, 2 * s2:2 * s2 + 2, :], pbf16[0:64, 768:1024].rearrange("p (s t) -> p s t", s=2),
                   [PBK[7]], [B("KcT")])
                dma(cstg[:, 128:256].rearrange("p (s c) -> p s c", s=2),
                    cv_d[2 * s2:2 * s2 + 2, :, h * 64:(h + 1) * 64].rearrange("s k c -> k s c"), writes=[B("cstgv")])
                cp("dve", Vc[:, 2 * s2:2 * s2 + 2, :], cstg[:, 128:256].rearrange("p (s c) -> p s c", s=2),
                   [B("cstgv")], [B("Vc")])
                yield
            yield
        for g in range(4):
            qT_g = v["qaT"][0:64, g, tok]
            if kind == "s":
                tt("dve", mskb[0:64], qT_g.unsqueeze(1).to_broadcast([64, 16, 128]), colm[0:64], ALU.mult,
                   PHB + [B("colm")], [B("mskb")])
                for s in range(16):
                    mm(pbank[6][:, 0:128], mskb[0:64, s, :], KcT[0:64, s, :], s == 0, s == 15, [B("mskb"), B("KcT")],
                       [PBK[6]], sig=False)
            else:
                kprev = kaT_c[0:64, h, :] if j == 0 else v["kaT"][0:64, j - 1, :]
                mm(pbank[6][:, 0:128], qT_g, kprev, True, True, PHB + [B(f"kaTc{h}")], [PBK[6]], sig=False)
            mm(pbank[6][:, 128:256], qT_g, v["kaT"][0:64, j, :], True, True, PHB, [PBK[6]])
            stt(sc[:], pbank[6][:, 0:256], 0.125, amask[:, mi, :], ALU.mult, ALU.add, [PBK[6], B("amask")], AT)
            em.op("dve", lambda e: e.reduce_max(out=asml[:, 0:1], in_=sc[:], axis=AX.X), AT, [B("asml")])
            si = h * 4 + g
            ts("dve", asml[:, 1:2], asml[:, 0:1], sinks[:, si:si + 1], ALU.max, [B("asml"), B("sinks")], [B("asml")],
               s2=-1.0, op1=ALU.mult)
            act(pbf[:], sc[:], AF.Exp, AT + [B("asml")], [B("pbf"), B("asml2")], bias=asml[:, 1:2], accum=asml[:, 4:5])
            act(asml[:, 2:3], sinks[:, si:si + 1], AF.Exp, [B("sinks"), B("asml")], [B("asml3")], bias=asml[:, 1:2])
            tt("dve", asml[:, 3:4], asml[:, 2:3], asml[:, 4:5], ALU.add, [B("asml2"), B("asml3")], [B("asml4")])
            em.op("dve", lambda e: e.reciprocal(out=asml[:, 5:6], in_=asml[:, 3:4]), [B("asml4")], [B("asml4")])
            yield
            for c in range(2):
                tr(pbf16[:, 768 + c * 128:768 + (c + 1) * 128], pbf[:, c * 128:(c + 1) * 128], ident_b[:],
                   [B("pbf")] + CONST, [PBK[7]], sig=(c == 1))
            cp("act", pT[:], pbf16[:, 768:1024].rearrange("p (c t) -> p c t", c=2), [PBK[7]], [B("pT")])
            yield
            if kind == "s":
                tt("dve", mskb[:], pT[:, 0, :].unsqueeze(1).to_broadcast([128, 16, 128]), colm[:], ALU.mult,
                   [B("pT"), B("colm")], [B("mskb")])
                for s in range(16):
                    mm(pbank[6][:, 256:320], mskb[:, s, :], Vc[:, s, :], s == 0, False, [B("mskb"), B("Vc")], [PBK[6]],
                       sig=False)
            else:
                vprev = va_c[:, h, :] if j == 0 else v["va"][:, j - 1, :]
                mm(pbank[6][:, 256:320], pT[:, 0, :], vprev, True, False, [B("pT"), B(f"vac{h}")] + PHB, [PBK[6]], sig=False)
            mm(pbank[6][:, 256:320], pT[:, 1, :], v["va"][:, j, :], False, True, [B("pT")] + PHB, [PBK[6]])
            stt(mixa[:, g * 64:(g + 1) * 64], pbank[6][:, 256:320], asml[:, 5:6], v["za"][:, j, g * 64:(g + 1) * 64],
                ALU.mult, ALU.mult, [PBK[6], B("asml4")] + PHB, [B("mixa")])
            yield
        for dc in range(2):
            tr(pbf16[:, 768 + dc * 128:768 + (dc + 1) * 128], mixa[:, dc * 128:(dc + 1) * 128], ident_b[:],
               [B("mixa")] + CONST, [PBK[7]], sig=(dc == 1))
        cp("act", mixT[:, KCM + 2 * h:KCM + 2 * h + 2, tok], pbf16[:, 768:1024].rearrange("p (c t) -> p c t", c=2),
           [PBK[7]], [B("mixT")])
        yield

    def make_head_tasks(tiles, h, mode):
        par = h % 2
        v = phv[par]
        PHB = [B(f"ph{par}")]
        nt = len(tiles)
        out = []

        def wcols(c0, n):
            return wpiece("w_in", c0, n, KC)

        def ev_q(j, ps, pb):
            cp("act", evb[:], ps[:, 0:256], [pb], [B("evb")])
            for dc in range(2):
                tr(pbf16[:, dc * 128:(dc + 1) * 128], evb[:, dc * 128:(dc + 1) * 128], ident_b[:], [B("evb")] + CONST,
                   [PBK[7]], sig=(dc == 1))
            cp("act", v["qT"][:, :, j * 128:(j + 1) * 128], pbf16[:, 0:256].rearrange("p (c t) -> p c t", c=2), [PBK[7]], PHB)

        def ev_k(j, ps, pb):
            act(v["km"][:, j, :], ps[:, 0:256], AF.Copy, [pb], PHB, scale=1.0 / 16.0)
            if mode == "main":
                for dc in range(2):
                    tr(pbf16[:, 256 + dc * 128:256 + (dc + 1) * 128], v["km"][:, j, dc * 128:(dc + 1) * 128], ident_b[:],
                       PHB + CONST, [PBK[7]], sig=(dc == 1))
                cp("act", v["kT"][:, :, j * 128:(j + 1) * 128], pbf16[:, 256:512].rearrange("p (c t) -> p c t", c=2),
                   [PBK[7]], PHB)

        def ev_v(j, ps, pb):
            cp("act", v["vaug"][:, j, 0:256], ps[:, 0:256], [pb], PHB)

        def ev_o(j, ps, pb):
            act(v["sgo"][:, j, :], ps[:, 0:256], AF.Sigmoid, [pb], PHB)

        def ev_z(j, ps, pb):
            act(ztmp[:], ps[:, 0:256], AF.Silu, [pb], [B("ztmp")])
            tt("dve", v["gz"][:, j, :], ztmp[:], ngh[par][:], ALU.mult, [B("ztmp"), B(f"ngh{par}")], PHB)

        def rope_apply(dst, src, ng, ti, rd, wr):
            cosb = rope[:, ti, 0, :].unsqueeze(1).to_broadcast([128, ng, 64])
            t3 = ropet[:, 0:ng * 64].rearrange("p (g e) -> p g e", g=ng)
            tt("dve", t3, src, cosb, ALU.mult, rd + [B("rope")], [B("ropet")])
            sn1 = rope[:, ti, 1, 0:32].unsqueeze(1).to_broadcast([128, ng, 32])
            sn2 = rope[:, ti, 1, 32:64].unsqueeze(1).to_broadcast([128, ng, 32])
            u3 = ropet[:, 256:256 + ng * 64].rearrange("p (g e) -> p g e", g=ng)
            tt("dve", u3[:, :, 0:32], src[:, :, 32:64], sn1, ALU.mult, rd + [B("rope")], [B("ropeu")])
            tt("dve", u3[:, :, 32:64], src[:, :, 0:32], sn2, ALU.mult, rd + [B("rope")], [B("ropeu")])
            tt("dve", dst, t3, u3, ALU.add, [B("ropet"), B("ropeu")], wr)

        def tile_rope_idx(j):
            return j

        def ev_qa(j, ps, pb):
            rope_apply(qrot[:].rearrange("p (g e) -> p g e", g=4), ps[:, 0:256].rearrange("p (g e) -> p g e", g=4), 4,
                       tile_rope_idx(j), [pb], [B("qrot")])
            for g in range(4):
                tr(pbf16[0:64, g * 128:(g + 1) * 128], qrot[:, g * 64:(g + 1) * 64], ident_b[:], [B("qrot")] + CONST,
                   [PBK[7]], sig=(g == 3))
            cp("act", v["qaT"][0:64, :, j * 128:(j + 1) * 128], pbf16[0:64, 0:512].rearrange("p (g t) -> p g t", g=4),
               [PBK[7]], PHB)

        def ev_kv(j, ps, pb):
            kind, idx = tiles[j]
            cp("act", kvf[:], ps[:, 0:128], [pb], [B("kvf")])
            rope_apply(krot[:].unsqueeze(1), kvf[:, 0:64].unsqueeze(1), 1, tile_rope_idx(j), [B("kvf")], [B("krot")])
            cp("dve", krb[:], krot[:], [B("krot")], [B("krb")])
            tr(pbf16[0:64, 0:128], krb[:], ident_b[:], [B("krb")] + CONST, [PBK[7]])
            if mode == "pre":
                cp("dve", kaT_c[0:64, h, :], pbf16[0:64, 0:128], [PBK[7]], [B(f"kaTc{h}")])
                cp("act", va_c[:, h, :], kvf[:, 64:128], [B("kvf")], [B(f"vac{h}")])
                return
            cp("act", v["kaT"][0:64, j, :], pbf16[0:64, 0:128], [PBK[7]], PHB)
            cp("act", v["va"][:, j, :], kvf[:, 64:128], [B("kvf")], PHB)
            if kind == "p" and idx == NPT - 1:
                dma(pk_d[:, h * 64:(h + 1) * 64], krot[:], reads=[B("krot")])
                dma(pv_d[:, h * 64:(h + 1) * 64], kvf[:, 64:128], reads=[B("kvf")])
            if kind == "s":
                dma(ok_d[:, 120:128, h * 64:(h + 1) * 64], krot[:], reads=[B("krot")])
                dma(ov_d[:, 120:128, h * 64:(h + 1) * 64], kvf[:, 64:128], reads=[B("kvf")])

        def ev_za(j, ps, pb):
            act(v["za"][:, j, :], ps[:, 0:256], AF.Silu, [pb], PHB)

        def mk(c0, n, ev, pre=None):
            def run(wslot, tick):
                if pre:
                    pre()
                proj_tiles(wslot, KC, n, nt, ev, tick)
            tk = Task([(wcols(c0, n), KC, n, 0)], run)
            tk.nticks = nt
            tk.nhooks = nt * (KC // 8 - 1)
            return tk

        def run_kv(wslot, tick):
            if mode == "pre":
                lastj = nt - 1
                bk = pj["bank"] % 3
                pj["bank"] += 1
                for kc in range(KC):
                    mm(pbank[bk][:, 0:128], xT[:, kc, lastj * 128:(lastj + 1) * 128], wbf[wslot][:, kc, 0:128], kc == 0,
                       kc == KC - 1, [B("xT"), wb(wslot, kc)], [PBK[bk]], sig=(kc == KC - 1))
                ev_kv(lastj, pbank[bk], PBK[bk])
                tick()
            else:
                proj_tiles(wslot, KC, 128, nt, ev_kv, tick)

        def load_ng():
            dma(ngh[par][:], ng_d[:, h * 256:(h + 1) * 256], writes=[B(f"ngh{par}")])

        if mode == "main":
            out.append(mk(cfg["c_qm"] + h * 256, 256, ev_q, pre=load_ng))
        out.append(mk(cfg["c_km"] + h * 256, 256, ev_k))
        out.append(mk(cfg["c_vm"] + h * 256, 256, ev_v))
        if mode == "main":
            out.append(mk(cfg["c_om"] + h * 256, 256, ev_o))
            out.append(mk(cfg["c_zm"] + h * 256, 256, ev_z))
            out.append(mk(cfg["c_qa"] + h * 256, 256, ev_qa))
        kvt = Task([(wcols(cfg["c_ka"] + h * 64, 64), KC, 64, 0), (wcols(cfg["c_va"] + h * 64, 64), KC, 64, 64)], run_kv)
        kvt.nticks = 1 if mode == "pre" else nt
        kvt.nhooks = 0 if mode == "pre" else nt * (KC // 8 - 1)
        if mode == "main" or tiles[-1] == ("x", NPT - 1):
            out.append(kvt)
        if mode == "main":
            out.append(mk(cfg["c_za"] + h * 256, 256, ev_za))
        return out

    def group_tasks(tiles, mode, gi):
        nt = len(tiles)
        tl = []

        def setup(wslot, tick):
            drain()
            for par in range(2):
                em.op("dve", lambda e, par=par: e.memset(phv[par]["vaug"][:, :, 256:257], 1.0), [], [B(f"ph{par}")])
            for j, (kind, idx) in enumerate(tiles):
                src = xp_d[idx] if kind == "x" else (x_d[NPT] if kind == "s" else x_d[idx])
                load_x_tile(src, j)
                ri = NPT if kind == "s" else (NT if kind == "x" else idx)
                dma(rope[:, j, :, :], rope_d[:, ri, :, :], writes=[B("rope")])

            def ev_g(j, ps, pb):
                cp("dve", gl[:, j, :], ps[:, 0:2 * NH], [pb], [B("gates")])
            proj_tiles(wslot, KC, 2 * NH, nt, ev_g, tick)
            gates_math(tiles)

        ts_ = Task([(wpiece("w_in", cfg["c_i"], 2 * NH, KC), KC, 2 * NH, 0)], setup)
        ts_.nticks = nt
        ts_.nhooks = nt * (KC // 8 - 1)
        tl.append(ts_)

        for h in range(NH):
            ht = make_head_tasks(tiles, h, mode)
            last = ht[-1]

            def wrap(run, h=h):
                def run2(wslot, tick):
                    run(wslot, tick)
                    drain()
                    par = h % 2

                    def mixer_gen():
                        if mode == "main":
                            cp("act", cbf[h % 2][:], cst[:, h, :, :], [B(f"cst{h}")], [B(f"cbf{h % 2}")])
                        for j, (kind, idx) in enumerate(tiles):
                            yield from mlstm_chunk(kind, j, h, par, idx)

                    def attn_gen():
                        for j, (kind, idx) in enumerate(tiles):
                            yield from attn_units(kind, j, h, par, kind == "p" and idx == 0)
                        lastp = [j for j, (k, _) in enumerate(tiles) if k == "p"]
                        if lastp:
                            jl = lastp[-1]
                            v = phv[par]
                            cp("dve", kaT_c[0:64, h, :], v["kaT"][0:64, jl, :], [B(f"ph{par}")], [B(f"kaTc{h}")])
                            cp("dve", va_c[:, h, :], v["va"][:, jl, :], [B(f"ph{par}")], [B(f"vac{h}")])
                        yield

                    gens.append(mixer_gen())
                    if mode == "main":
                        gens.append(attn_gen())
                return run2
            last.run = wrap(last.run)
            tl.extend(ht)

        if mode == "pre":
            return tl

        T = nt * 128
        for cb in range(KC):
            def run_m(wslot_a, tick, cb=cb):
                pass
            def mk_half(cb, half):
                gcol = cfg["c_gm"] if half == 0 else cfg["c_ga"]
                wsrc = "w_bm" if half == 0 else "w_ba"
                pieces = [(wpiece("w_in", gcol + cb * 128, 128, KC), KC, 128, 0),
                          (wpiece(wsrc, cb * 128, 128, KCM), KCM, 128, 128)]
                bg, bb = (3, 5) if half == 0 else (4, 6)

                def run_h(wslot, tick):
                    if cb == 0 and half == 0:
                        drain()
                    for kc in range(KC):
                        mm(pbank[bg][:, 0:T], wbf[wslot][:, kc, 0:128], xT[:, kc, 0:T], kc == 0,
                           kc == KC - 1, [B("xT"), wb(wslot, kc)], [PBK[bg]], sig=(kc == KC - 1))
                        if kc % 8 == 7 and kc != KC - 1:
                            mini()
                    act(misc[:, half * 512:half * 512 + T], pbank[bg][:, 0:T], AF.Sigmoid, [PBK[bg]], [B(f"msig{half}")])
                    tick()
                    for kc in range(KCM):
                        mm(pbank[bb][:, 0:T], wbf[wslot][:, kc, 128:256], mixT[:, half * KCM + kc, 0:T],
                           kc == 0, kc == KCM - 1, [B("mixT"), wb(wslot, kc)], [PBK[bb]], sig=(kc == KCM - 1))
                        if kc % 8 == 7 and kc != KCM - 1:
                            mini()
                    tt("dve", misc[:, 1024 + half * 512:1024 + half * 512 + T], pbank[bb][:, 0:T],
                       misc[:, half * 512:half * 512 + T], ALU.mult, [PBK[bb], B(f"msig{half}")], [B(f"mprod{half}")])
                    tick()
                    if half == 1:
                        wr = [B("mergedT")] + ([B("ph0"), B("ph1")] if cb == 0 else [])
                        tt("dve", mergedT[:, cb, 0:T], misc[:, 1024:1024 + T], misc[:, 1536:1536 + T], ALU.add,
                           [B("mprod0"), B("mprod1")], wr)
                tk = Task(pieces, run_h)
                tk.nticks = 2
                tk.nhooks = (KC // 8 - 1) + max(0, KCM // 8 - 1)
                return tk
            tl.append(mk_half(cb, 0))
            tl.append(mk_half(cb, 1))

        NOB = D // 256
        for ob in range(NOB):
            def run_o(wslot, tick, ob=ob):
                if ob == 0:
                    for j, (kind, idx) in enumerate(tiles):
                        src = x_d[NPT] if kind == "s" else x_d[idx]
                        dma(ypre[:, j, :], src, writes=[B(f"ypre{j}"), B("xT"), B("mixT")])

                def ev(j, ps, pb):
                    stt(ypre[:, j, ob * 256:(ob + 1) * 256], ypre[:, j, ob * 256:(ob + 1) * 256], alpha, ps[:, 0:256],
                        ALU.mult, ALU.add, [pb], [B(f"ypre{j}")])
                proj_tiles(wslot, KC, 256, nt, ev, tick, lhs=mergedT, lhs_buf=[B("mergedT"), B("ph0"), B("ph1")])
            to_ = Task([(wpiece("w_out", ob * 256, 256, KC), KC, 256, 0)], run_o)
            to_.nticks = nt
            to_.nhooks = nt * (KC // 8 - 1)
            tl.append(to_)

        def run_ln(wslot, tick):
            nch = max(1, D // 512)
            cw = D // nch
            for j, (kind, idx) in enumerate(tiles):
                YB = [B(f"ypre{j}")]
                for c in range(nch):
                    E("dve", "bn_stats", YB, [B("stats")], out=stats[:, c, :], in_=ypre[:, j, c * cw:(c + 1) * cw])
                em.op("dve", lambda e: e.bn_aggr(out=sml[:, 8:10], in_=stats[:, 0:nch, :].rearrange("p c s -> p (c s)")),
                      [B("stats")], [B("sml")])
                act(sml[:, 10:11], sml[:, 9:10], AF.Ln, [B("sml")], [B("sml")], bias=LN_EPS)
                act(sml[:, 10:11], sml[:, 10:11], AF.Exp, [B("sml")], [B("sml")], scale=-0.5)
                ts("dve", ypre[:, j, :], ypre[:, j, :], sml[:, 8:9], ALU.subtract, YB + [B("sml")], YB, s2=sml[:, 10:11],
                   op1=ALU.mult)
                for c0 in range(0, D, 1024):
                    w = min(1024, D - c0)
                    MW_ = [B("misc"), B("msig0"), B("msig1"), B("mprod0"), B("mprod1")]
                    dma(misc[:, 0:w], lng_d[:, c0:c0 + w], writes=MW_)
                    dma(misc[:, 1024:1024 + w], lnb_d[:, c0:c0 + w], writes=MW_)
                    tt("dve", ypre[:, j, c0:c0 + w], ypre[:, j, c0:c0 + w], misc[:, 0:w], ALU.mult, YB + [B("misc")], YB)
                    tt("dve", ypre[:, j, c0:c0 + w], ypre[:, j, c0:c0 + w], misc[:, 1024:1024 + w], ALU.add,
                       YB + [B("misc")], YB)
                didx = NPT if kind == "s" else idx
                dma(y_d[didx], ypre[:, j, :], reads=YB + [B("xT"), B("mixT")])
        tl.append(Task([], run_ln))
        return tl

    def chunks(lst, n):
        return [lst[i:i + n] for i in range(0, len(lst), n)]

    pre_groups = chunks([("x", i) for i in range(NPT)], GT)
    main_groups = chunks([("p", i) for i in range(NPT)] + [("s", 0)], GT)
    for gi, tiles in enumerate(pre_groups):
        tasks.extend(group_tasks(tiles, "pre", gi))

    def apply_flag(wslot, tick):
        drain()
        allc = [B(f"cst{h}") for h in range(NH)]
        ts("dve", cst[:].rearrange("p h c e -> p (h c e)"), cst[:].rearrange("p h c e -> p (h c e)"), flag[:, 0:1],
           ALU.mult, allc + [B("flag")], allc)
        ts("dve", gsm[:, 1:2], gsm[:, 1:2], flag[0:NH, 0:1], ALU.mult, [B("gsm"), B("flag")], [B("gsm")])
        em.op("dve", lambda e: e.memset(gsm[:, 0:1], 0.0), [], [B("gsm")])
    tasks.append(Task([], apply_flag))
    for gi, tiles in enumerate(main_groups):
        tasks.extend(group_tasks(tiles, "main", gi))

    def finale(wslot, tick):
        drain()
        allc = [B(f"cst{h}") for h in range(NH)]
        for h in range(NH):
            for dc in range(2):
                dma(pC_d[h, dc * 128:(dc + 1) * 128, :], cst[:, h, dc, 0:256], reads=[B(f"cst{h}")])
        cp("dve", hot[:, 0:2 * NH].rearrange("p (h c) -> p h c", c=2), cst[:, :, :, 256], allc, [B("hot")])
        tr(pbank[3][0:2 * NH, 0:128], hot[:, 0:2 * NH], ident_f[:], [B("hot")] + CONST, [PBK[3]])
        cp("dve", tmpm[0:2 * NH, 0:128], pbank[3][0:2 * NH, 0:128], [PBK[3]], [B("tmpm")])
        dma(pn_d.rearrange("h (c d) -> (h c) d", c=2), tmpm[0:2 * NH, 0:128], reads=[B("tmpm")])
        dma(pm_d, gsm[:, 1:2], reads=[B("gsm")])
        dma(om_d.rearrange("b h -> h b"), smo[:], reads=[B("smo")], allow_slow_non_contiguous=True)
        dma(ok_d[:, 0:120, :], ck_d[:, 8:128, :], reads=[B("ckshift")])
        dma(ov_d[:, 0:120, :], cv_d[:, 8:128, :], reads=[B("cvshift")])
    tasks.append(Task([], finale))

    def run_all():
        n = len(tasks)
        for i, t in enumerate(tasks):
            t.load_all()
            nxt = tasks[i + 1] if i + 1 < n else None
            if nxt is not None:
                nxt.load_all()

            nticks = max(1, getattr(t, "nticks", 3))
            hook["nxt"] = nxt
            nhooks = getattr(t, "nhooks", 0)
            quota = 0 if nxt is None else max(0, -(-(len(nxt.slabs) - nhooks) // nticks))

            def tick(nxt=nxt, quota=quota):
                if nxt is not None:
                    for _ in range(quota):
                        nxt.load_one()
                advance(2)
            t.run(t.wslot if t.wslot is not None else 0, tick)
    run_all()
    drain()
    em.finish()
    em.replay()
    st.close()
    return nc


def _consts(cfg, half):
    NPT, NH = cfg["NPT"], cfg["NH"]
    NT = NPT + 1
    TP = NPT * 128
    p = np.arange(128)
    t = p[:, None]
    j = np.arange(128)[None, :]
    neg = np.float32(NEG)
    z = np.float32(0)
    amask = np.full((128, 3, 256), neg, np.float32)
    amask[:, 0, 0:128] = np.where(j > t, z, neg)
    amask[:, 0, 128:256] = np.where(j <= t, z, neg)
    amask[:, 1] = amask[:, 0]
    if half == 0:
        amask[:, 1, 0:128] = neg
    tl = t % 8
    amask[:, 2, 0:128] = np.where(j > tl, z, neg)
    amask[:, 2, 128:256] = np.where((j // 8 == t // 8) & (j <= t), z, neg)
    dmask = np.full((128, 2, 128), neg, np.float32)
    dmask[:, 0, :] = np.where(t <= j, z, neg)
    dmask[:, 1, :] = np.where((t <= j) & (t // 8 == j // 8), z, neg)
    rope = np.zeros((128, NT + 1, 2, 64), np.float32)
    inv = (np.float32(ROPE_THETA) ** (-(np.arange(32, dtype=np.float32)) / np.float32(32))).astype(np.float32)
    for i in range(NT + 1):
        if i < NPT:
            pos = half * TP + i * 128 + p
        elif i == NPT:
            pos = PAST_LEN + p % 8
        else:
            pos = TP - 128 + p
        ang = (pos.astype(np.float32)[:, None] * inv[None, :]).astype(np.float32)
        c = np.cos(ang).astype(np.float32)
        s = np.sin(ang).astype(np.float32)
        rope[:, i, 0, 0:32] = c
        rope[:, i, 0, 32:64] = c
        rope[:, i, 1, 0:32] = -s
        rope[:, i, 1, 32:64] = s
    seqm = (p[:, None] // 8 == np.arange(16)[None, :]).astype(np.float32)
    colm = np.zeros((128, 16, 128), np.float32)
    colm[:, :, :] = (np.arange(128)[None, :] // 8 == np.arange(16)[:, None]).astype(np.float32)[None]
    rst = np.zeros((NH, 2, 128), np.float32)
    rst[:, 0, :] = np.where(np.arange(128) % 8 == 0, 0.0, 1.0)
    rst[:, 1, :] = np.where(np.arange(128) % 8 == 0, -1e30, 0.0)
    return dict(ident=np.eye(128, dtype=np.float32), amask=amask, dmask=dmask, rope=rope, seqm=seqm,
                colm=colm.reshape(128, 16 * 128).astype(ml_dtypes.bfloat16), rst=rst,
                flag=np.full((128, 1), float(half), np.float32))


def shard_inputs(cfg, inp, ncores):
    D, NPT, NH = cfg["D"], cfg["NPT"], cfg["NH"]
    TP = NPT * 128
    f = lambda a: np.ascontiguousarray(np.asarray(a, dtype=np.float32))
    rep = lambda v: np.ascontiguousarray(np.broadcast_to(f(v)[None, :], (128, v.shape[-1])))
    xpr, xs = f(inp["x_prompt"]), f(inp["x_sample"])
    KC = cfg["KC"]
    wt = np.zeros((128, cfg["INW"] * KC + 2 * D * (KC // 2) + D * KC), np.float32)
    mats = {k: f(inp[k][0]) for k in ("w_in", "w_bm", "w_ba", "w_out")}
    for (mat, c0, n, nkc), off in cfg["_pieces"].items():
        blk = mats[mat][0:nkc * 128, c0:c0 + n].reshape(nkc, 128, n).transpose(1, 0, 2)
        wt[:, off:off + nkc * n] = blk.reshape(128, nkc * n)
    shared = dict(wt=wt,
                  bif=rep(inp["b_if"][0]), sinks=rep(inp["attn_sinks"][0]), ng=rep(inp["norm_m_g"][0]),
                  lng=rep(inp["ln_g"][0]), lnb=rep(inp["ln_b"][0]))
    maps = []
    for c in range(ncores):
        b, half = c // 2, c % 2
        m = dict(shared)
        x = np.empty((NPT + 1, 128, D), np.float32)
        x[0:NPT] = xpr[b, half * TP:(half + 1) * TP].reshape(NPT, 128, D)
        x[NPT] = xs[c * 16:(c + 1) * 16].reshape(128, D)
        m["x"] = x
        m["xp"] = (xpr[b, 0:TP].reshape(NPT, 128, D).copy() if half == 1 else np.zeros((NPT, 128, D), np.float32))
        m["sC"] = f(inp["state_C"][0, c * 16:(c + 1) * 16])
        m["sn"] = f(inp["state_n"][0, c * 16:(c + 1) * 16])
        m["sm"] = f(inp["state_m"][0, c * 16:(c + 1) * 16])
        m["ck"] = f(inp["cache_k"][0, c * 16:(c + 1) * 16]).reshape(16, 128, NH * 64)
        m["cv"] = f(inp["cache_v"][0, c * 16:(c + 1) * 16]).reshape(16, 128, NH * 64)
        m.update(_consts(cfg, half))
        maps.append(m)
    return maps


def gather_outputs(cfg, res, ncores):
    D, NPT, NH = cfg["D"], cfg["NPT"], cfg["NH"]
    TP = NPT * 128
    nb = ncores // 2
    yp = np.empty((nb, 2 * TP, D), np.float32)
    ys = np.empty((ncores * 16, 8, D), np.float32)
    pC = np.empty((1, nb, NH, 256, 256), np.float32)
    pn = np.empty((1, nb, NH, 256), np.float32)
    pm = np.empty((1, nb, NH), np.float32)
    pk = np.empty((1, nb, 128, NH, 64), np.float32)
    pv = np.empty((1, nb, 128, NH, 64), np.float32)
    oC = np.empty((1, ncores * 16, NH, 256, 256), np.float32)
    on = np.empty((1, ncores * 16, NH, 256), np.float32)
    om = np.empty((1, ncores * 16, NH), np.float32)
    ok = np.empty((1, ncores * 16, 128, NH, 64), np.float32)
    ov = np.empty((1, ncores * 16, 128, NH, 64), np.float32)
    for c in range(ncores):
        r = res[c]
        b, half = c // 2, c % 2
        yp[b, half * TP:(half + 1) * TP] = r["y"][0:NPT].reshape(TP, D)
        ys[c * 16:(c + 1) * 16] = r["y"][NPT].reshape(16, 8, D)
        if half == 1:
            pC[0, b] = r["pC"]
            pn[0, b] = r["pn"]
            pm[0, b] = r["pm"][:, 0]
            pk[0, b] = r["pk"].reshape(128, NH, 64)
            pv[0, b] = r["pv"].reshape(128, NH, 64)
        sl = slice(c * 16, (c + 1) * 16)
        oC[0, sl] = r["oC"]
        on[0, sl] = r["on"]
        om[0, sl] = r["om"]
        ok[0, sl] = r["ok"].reshape(16, 128, NH, 64)
        ov[0, sl] = r["ov"].reshape(16, 128, NH, 64)
    return (yp, ys, pC, pn, pm, pk, pv, oC, on, om, ok, ov)


_NC_CACHE = {}


def kernel(**inputs):
    key = "full"
    if key not in _NC_CACHE:
        cfg = make_cfg(D=4096, NPT=8, GT=3)
        _NC_CACHE[key] = (build(cfg), cfg)
    nc, cfg = _NC_CACHE[key]
    maps = shard_inputs(cfg, inputs, 8)
    res = run_bass_kernel_spmd(nc, maps, core_ids=list(range(8)))
    return gather_outputs(cfg, res.results, 8)
```

```python
import math
from contextlib import ExitStack

import numpy as np
import ml_dtypes
import concourse.bass as bass
import concourse.mybir as mybir
from concourse.bass_utils import run_bass_kernel_spmd

F32 = mybir.dt.float32
BF16 = mybir.dt.bfloat16
AF = mybir.ActivationFunctionType
ALU = mybir.AluOpType
AX = mybir.AxisListType

EPOCH = 16000
NEG = -30000.0
LN_EPS = 1e-5
ROPE_THETA = 10000.0
PAST_LEN = 8192


class Buf:
    __slots__ = ("name", "w", "r", "dsem", "dcnt", "psum")

    def __init__(self, name, psum=False):
        self.name = name
        self.psum = psum
        self.w = None
        self.r = {}
        self.dsem = None
        self.dcnt = 0


class Emitter:
    ENGS = ("pe", "dve", "act", "pool", "sp")

    def __init__(self, nc, stack):
        self.nc = nc
        self.stack = stack
        self.streams = {e: [] for e in self.ENGS}
        self.sem = {}
        self.cnt = {}
        self.semid = 0
        for e in self.ENGS:
            self.sem[e] = self._newsem(e)
            self.cnt[e] = 0
        self.pending = {e: False for e in self.ENGS}
        self.known = {e: {} for e in self.ENGS}
        self.dma_bufs = []
        self.tags = None

    def _newsem(self, tag):
        self.semid += 1
        return self.stack.enter_context(self.nc.semaphore(f"s{self.semid}_{tag}"))

    def _wait(self, eng, tok):
        if tok is None:
            return
        sem, val = tok
        k = self.known[eng]
        key = sem.name
        if k.get(key, 0) >= val:
            return
        if sem is self.sem[eng] and val > self.cnt[eng]:
            return
        k[key] = val
        self.streams[eng].append(("w", sem, val))

    def _deps(self, eng, reads, writes):
        for b in reads:
            self._wait(eng, b.w)
            if b.psum:
                own = self.sem[eng].name
                for k, t in list(b.r.items()):
                    if k != own:
                        self._wait(eng, t)
        for b in writes:
            self._wait(eng, b.w)
            for t in list(b.r.values()):
                self._wait(eng, t)

    def _mark(self, tok, reads, writes):
        for b in writes:
            b.w = tok
            b.r = {}
        for b in reads:
            if b not in writes:
                b.r[tok[0].name] = tok

    def op(self, eng, fn, reads=(), writes=(), sig=True):
        if self.tags is not None:
            import sys as _s
            fr = _s._getframe(2)
            self.tags[eng].append((fr.f_lineno, fr.f_back.f_lineno if fr.f_back else -1))
        self._deps(eng, reads, writes)
        if sig and self.cnt[eng] >= EPOCH and not self.pending[eng]:
            self.sem[eng] = self._newsem(eng)
            self.cnt[eng] = 0
        if sig:
            self.cnt[eng] += 1
            tok = (self.sem[eng], self.cnt[eng])
            self.streams[eng].append(("o", fn, self.sem[eng]))
            self.pending[eng] = False
        else:
            tok = (self.sem[eng], self.cnt[eng] + 1)
            self.streams[eng].append(("o", fn, None))
            self.pending[eng] = True
        self._mark(tok, reads, writes)
        return tok

    def dma(self, eng, out_ap, in_ap, reads=(), writes=(), **kw):
        self._deps(eng, reads, writes)
        main = (list(writes) + list(reads))[0]
        if main.dsem is None:
            main.dsem = self._newsem("d" + main.name)
            self.dma_bufs.append(main)
        main.dcnt += 16
        tok = (main.dsem, main.dcnt)
        self.streams[eng].append(("d", out_ap, in_ap, kw, main.dsem))
        self._mark(tok, reads, writes)
        return tok

    def finish(self):
        for b in self.dma_bufs:
            self._wait("sp", (b.dsem, b.dcnt))

    def replay(self):
        nc = self.nc
        streams = self.streams

        def run(h, items):
            for it in items:
                if it[0] == "w":
                    h.wait_ge(it[1], it[2])
                elif it[0] == "o":
                    ins = it[1](h)
                    if it[2] is not None:
                        ins.then_inc(it[2], 1)
                else:
                    h.dma_start(out=it[1], in_=it[2], **it[3]).then_inc(it[4], 16)

        with nc.Block() as block:
            @block.tensor
            def _(e):
                run(e, streams["pe"])

            @block.vector
            def _(e):
                run(e, streams["dve"])

            @block.scalar
            def _(e):
                run(e, streams["act"])

            @block.gpsimd
            def _(e):
                run(e, streams["pool"])

            @block.sync
            def _(e):
                run(e, streams["sp"])


def make_cfg(D=4096, NPT=8, GT=3):
    NH = D // 512
    MW = D // 2
    c = dict(D=D, NPT=NPT, GT=GT, NH=NH, MW=MW, KC=D // 128)
    o = 0
    for name, size in (("qm", MW), ("km", MW), ("vm", MW), ("om", MW), ("zm", MW), ("i", NH), ("f", NH),
                       ("qa", MW), ("ka", NH * 64), ("va", NH * 64), ("za", MW), ("gm", D), ("ga", D)):
        c["c_" + name] = o
        o += size
    c["INW"] = o
    c["alpha"] = 2.0 ** 0.25
    return c


def build(cfg):
    D, NPT, GT, NH, MW, KC = cfg["D"], cfg["NPT"], cfg["GT"], cfg["NH"], cfg["MW"], cfg["KC"]
    INW = cfg["INW"]
    NT = NPT + 1
    KCM = MW // 128
    SLK = min(4, KC)
    TG = GT * 128
    alpha = cfg["alpha"]

    nc = bass.Bass("TRN2", target_bir_lowering=False)

    def din(name, shape, dt=F32):
        return nc.dram_tensor(name, list(shape), dt, kind="ExternalInput").ap()

    def dout(name, shape, dt=F32):
        return nc.dram_tensor(name, list(shape), dt, kind="ExternalOutput").ap()

    x_d = din("x", [NT, 128, D])
    xp_d = din("xp", [NPT, 128, D])
    WT_TOTAL = INW * KC + 2 * D * KCM + D * KC
    wt_d = din("wt", [128, WT_TOTAL])
    pieces_reg = cfg.setdefault("_pieces", {})
    pieces_tot = [0]

    def wpiece(mat, c0, n, nkc):
        key = (mat, c0, n, nkc)
        if key not in pieces_reg:
            pieces_reg[key] = pieces_tot[0]
            pieces_tot[0] += n * nkc
            assert pieces_tot[0] <= WT_TOTAL
        return ("wt", pieces_reg[key])
    bif_d = din("bif", [128, 2 * NH])
    sinks_d = din("sinks", [128, 4 * NH])
    ng_d = din("ng", [128, MW])
    lng_d = din("lng", [128, D])
    lnb_d = din("lnb", [128, D])
    sC_d = din("sC", [16, NH, 256, 256])
    sn_d = din("sn", [16, NH, 256])
    sm_d = din("sm", [16, NH])
    ck_d = din("ck", [16, 128, NH * 64])
    cv_d = din("cv", [16, 128, NH * 64])
    ident_d = din("ident", [128, 128])
    amask_d = din("amask", [128, 3, 256])
    dmask_d = din("dmask", [128, 2, 128])
    rope_d = din("rope", [128, NT + 1, 2, 64])
    seqm_d = din("seqm", [128, 16])
    colm_d = din("colm", [128, 16 * 128], BF16)
    flag_d = din("flag", [128, 1])
    rst_d = din("rst", [NH, 2, 128])

    y_d = dout("y", [NT, 128, D])
    pC_d = dout("pC", [NH, 256, 256])
    pn_d = dout("pn", [NH, 256])
    pm_d = dout("pm", [NH, 1])
    pk_d = dout("pk", [128, NH * 64])
    pv_d = dout("pv", [128, NH * 64])
    oC_d = dout("oC", [16, NH, 256, 256])
    on_d = dout("on", [16, NH, 256])
    om_d = dout("om", [16, NH])
    ok_d = dout("ok", [16, 128, NH * 64])
    ov_d = dout("ov", [16, 128, NH * 64])

    dbg_d = dout("dbg", [128, KC * TG], BF16) if cfg.get("debug") else None
    st = ExitStack()
    em = Emitter(nc, st)
    bufs = {}

    def B(name, psum=False):
        if name not in bufs:
            bufs[name] = Buf(name, psum)
        return bufs[name]

    def SB(name, shape, dt=F32):
        return st.enter_context(nc.sbuf_tensor("sb_" + name, list(shape), dt))

    def PS(name, shape, dt=F32):
        return st.enter_context(nc.psum_tensor("ps_" + name, list(shape), dt))

    regA = SB("regA", [128, GT * D])
    regA_b = regA[:].bitcast(BF16)
    xT = regA_b[:, 0:KC * TG].rearrange("p (c t) -> p c t", c=KC)
    mixT = regA_b[:, KC * TG:2 * KC * TG].rearrange("p (c t) -> p c t", c=KC)
    ypre = regA[:].rearrange("p (j d) -> p j d", j=GT)
    wbf = [SB(f"wbf{i}", [128, KC, 256], BF16) for i in range(2)]
    misc = SB("misc", [128, 2048])
    cst = SB("cst", [128, NH, 2, 257])
    cbf = [SB(f"cbf{i}", [128, 2, 257], BF16) for i in range(2)]
    PH = 2 * TG + 2 * TG + GT * 256 + GT * 257 + GT * 256 + GT * 256 + 4 * TG + TG + GT * 64 + GT * 256
    PHT = max(2 * PH, KC * TG)
    phreg = SB("phreg", [128, PHT], BF16)
    mergedT = phreg[:, 0:KC * TG].rearrange("p (c t) -> p c t", c=KC)

    def ph_views(par):
        o = par * PH
        v = {}

        def take(name, n, pat=None, **kw):
            nonlocal o
            a = phreg[:, o:o + n]
            o += n
            v[name] = a.rearrange(pat, **kw) if pat else a

        take("qT", 2 * TG, "p (c t) -> p c t", c=2)
        take("kT", 2 * TG, "p (c t) -> p c t", c=2)
        take("km", GT * 256, "p (j e) -> p j e", j=GT)
        take("vaug", GT * 257, "p (j e) -> p j e", j=GT)
        take("sgo", GT * 256, "p (j e) -> p j e", j=GT)
        take("gz", GT * 256, "p (j e) -> p j e", j=GT)
        take("qaT", 4 * TG, "p (g t) -> p g t", g=4)
        take("kaT", TG, "p (j t) -> p j t", j=GT)
        take("va", GT * 64, "p (j e) -> p j e", j=GT)
        take("za", GT * 256, "p (j e) -> p j e", j=GT)
        return v

    phv = [ph_views(0), ph_views(1)]
    kaT_c = SB("kaT_c", [128, NH, 128], BF16)
    va_c = SB("va_c", [128, NH, 64], BF16)
    ident_f = SB("ident_f", [128, 128])
    ident_b = SB("ident_b", [128, 128], BF16)
    amask = SB("amask", [128, 3, 256])
    dmask = SB("dmask", [128, 2, 128])
    rope = SB("rope", [128, GT, 2, 64])
    seqm = SB("seqm", [128, 16])
    colm = SB("colm", [128, 16, 128], BF16)
    flag = SB("flag", [128, 1])
    rst = SB("rst", [NH, 2, 128])
    bif = SB("bif", [128, 2 * NH])
    sinks = SB("sinks", [128, 4 * NH])
    ngh = [SB(f"ngh{i}", [128, 256]) for i in range(2)]
    ones_nh = SB("ones_nh", [NH, 128])
    ones_bf = SB("ones_bf", [128, 1], BF16)
    gl = SB("gl", [128, GT, 2 * NH])
    gA = SB("gA", [NH, TG])
    gBm = SB("gBm", [NH, TG])
    gC = SB("gC", [NH, TG])
    gD = SB("gD", [NH, TG])
    gW = SB("gW", [NH, TG])
    gsm = SB("gsm", [NH, 64])
    bcsrc = SB("bcsrc", [NH, 160])
    gtm = SB("gtm", [128, GT, 4, NH])
    DTp = SB("DTp", [128, 128])
    DTe = SB("DTe", [128, 128])
    SDT = SB("SDT", [128, 128], BF16)
    P1s = SB("P1s", [128, 257])
    ndt = SB("ndt", [128, 257])
    hot = SB("hot", [128, 256])
    tmpm = SB("tmpm", [128, 256])
    kw = SB("kw", [128, 256], BF16)
    mixtm = SB("mixtm", [128, 256], BF16)
    sml = SB("sml", [128, 32])
    bcs = SB("bcs", [128, 32])
    stats = SB("stats", [128, 8, 6])
    sc = SB("sc", [128, 256])
    pbf = SB("pbf", [128, 256], BF16)
    pT = SB("pT", [128, 2, 128], BF16)
    mixa = SB("mixa", [128, 256], BF16)
    asml = SB("asml", [128, 16])
    ztmp = SB("ztmp", [128, 256])
    cstg = SB("cstg", [128, 384])
    smo = SB("smo", [NH, 16])
    ropet = SB("ropet", [128, 512])
    qrot = SB("qrot", [128, 256], BF16)
    kvf = SB("kvf", [128, 128])
    krot = SB("krot", [128, 64])
    krb = SB("krb", [128, 64], BF16)
    evb = SB("evb", [128, 256], BF16)
    mskb = SB("mskb", [128, 16, 128], BF16)
    mqf = SB("mqf", [128, 2, 128])
    wm16 = SB("wm16", [128, 16])
    nall = SB("nall", [128, 2, 16])
    nout = SB("nout", [128, 2, 16])
    kw2 = SB("kw2", [128, 2, 128], BF16)
    KcT = SB("KcT", [128, 16, 128], BF16)
    Vc = SB("Vc", [128, 16, 64], BF16)
    NCS = 6
    csf = [SB(f"csf{i}", [128, 257]) for i in range(NCS)]

    sgA = SB("sgA", [NH, 16])
    sgB = SB("sgB", [NH, 16])

    pbank = [PS(f"pb{i}", [128, 512]) for i in range(7)]
    pbf16 = PS("pb7", [128, 1024], BF16)
    PBK = [B(f"pb{i}", psum=True) for i in range(8)]

    def wb(slot, kc):
        return B(f"wbf{slot}_{kc // SLK}")

    def mm(out, lhsT, rhs, start, stop, reads, writes, sig=True):
        em.op("pe", lambda e: e.matmul(out, lhsT=lhsT, rhs=rhs, start=start, stop=stop), reads, writes, sig)

    def tr(out, in_, ident, reads, writes, sig=True):
        em.op("pe", lambda e: e.transpose(out=out, in_=in_, identity=ident), reads, writes, sig)

    def act(out, in_, func, reads, writes, bias=None, scale=None, accum=None):
        kw_ = {}
        if bias is not None:
            kw_["bias"] = bias
        if scale is not None:
            kw_["scale"] = scale
        if accum is not None:
            kw_["accum_out"] = accum
        em.op("act", lambda e: e.activation(out=out, in_=in_, func=func, **kw_), reads, writes)

    def tt(eng, out, a, b, op, reads, writes):
        em.op(eng, lambda e: e.tensor_tensor(out=out, in0=a, in1=b, op=op), reads, writes)

    def ts(eng, out, a, s1, op0, reads, writes, s2=None, op1=None):
        if op1 is None:
            em.op(eng, lambda e: e.tensor_scalar(out=out, in0=a, scalar1=s1, scalar2=None, op0=op0), reads, writes)
        else:
            em.op(eng, lambda e: e.tensor_scalar(out=out, in0=a, scalar1=s1, scalar2=s2, op0=op0, op1=op1),
                  reads, writes)

    def stt(out, in0, scalar, in1, op0, op1, reads, writes):
        em.op("dve", lambda e: e.scalar_tensor_tensor(out=out, in0=in0, scalar=scalar, in1=in1, op0=op0, op1=op1),
              reads, writes)

    def cp(eng, out, in_, reads, writes):
        if eng == "act":
            em.op("act", lambda e: e.copy(out=out, in_=in_), reads, writes)
        else:
            em.op(eng, lambda e: e.tensor_copy(out=out, in_=in_), reads, writes)

    def E(eng, method, reads, writes, *args, **kwargs):
        em.op(eng, lambda e: getattr(e, method)(*args, **kwargs), reads, writes)

    def dma(out, in_, reads=(), writes=(), eng="sp", **kw_):
        em.dma(eng, out, in_, reads, writes, **kw_)

    for t, d, nm in ((ident_f, ident_d, "ident_f"), (amask, amask_d, "amask"), (dmask, dmask_d, "dmask"),
                     (seqm, seqm_d, "seqm"), (flag, flag_d, "flag"), (rst, rst_d, "rst"),
                     (bif, bif_d, "bif"), (sinks, sinks_d, "sinks")):
        dma(t[:], d, writes=[B(nm)])
    dma(colm[:].rearrange("p s t -> p (s t)"), colm_d, writes=[B("colm")])
    cp("dve", ident_b[:], ident_f[:], [B("ident_f")], [B("ident_b")])
    em.op("dve", lambda e: e.memset(ones_nh[:], 1.0), [], [B("ones_nh")])
    em.op("dve", lambda e: e.memset(ones_bf[:], 1.0), [], [B("ones_bf")])
    em.op("dve", lambda e: e.memset(cst[:], 0.0), [], [B("cst")])
    em.op("dve", lambda e: e.memset(gsm[:], 0.0), [], [B("gsm")])
    CONST = [B("ident_f"), B("ident_b")]

    wstate = {"slab": 0, "blk": 0}

    class Task:
        def __init__(self, pieces, run):
            self.pieces = pieces
            self.run = run
            self.slabs = []
            self.wslot = None
            for pi, (ap, nkc, ncols, coff) in enumerate(pieces):
                for k0 in range(0, nkc, SLK):
                    self.slabs.append((ap, k0, min(SLK, nkc - k0), ncols, coff))
            self.next = 0

        def load_one(self):
            if self.next >= len(self.slabs):
                return False
            if self.wslot is None:
                self.wslot = wstate["blk"] % 2
                wstate["blk"] += 1
            ap, k0, nk, ncols, coff = self.slabs[self.next]
            self.next += 1
            off = ap[1]
            src = wt_d[:, off + k0 * ncols:off + (k0 + nk) * ncols].rearrange("p (k n) -> p k n", k=nk)
            dma(wbf[self.wslot][:, k0:k0 + nk, coff:coff + ncols], src, writes=[wb(self.wslot, k0)], eng="pool")
            return True

        def load_all(self):
            while self.load_one():
                pass

    tasks = []
    gens = []

    rr = [0]

    def advance(n=1):
        for _ in range(n):
            if not gens:
                return
            rr[0] = (rr[0] + 1) % len(gens)
            g = gens[rr[0]]
            try:
                next(g)
            except StopIteration:
                gens.remove(g)

    def drain():
        while gens:
            advance()

    def load_x_tile(src_tile_ap, j):
        W4 = min(1024, D)
        for q4 in range(D // W4):
            hb = q4 % 2
            XB = B(f"miscx{hb}")
            base = hb * 1024
            dma(misc[:, base:base + W4], src_tile_ap[:, q4 * W4:(q4 + 1) * W4],
                writes=[XB, B("misc"), B("msig0"), B("msig1"), B("mprod0"), B("mprod1")])
            for q in range(W4 // 512):
                bk = 3 + (q % 2)
                for i in range(4):
                    tr(pbank[bk][:, i * 128:(i + 1) * 128],
                       misc[:, base + q * 512 + i * 128: base + q * 512 + (i + 1) * 128],
                       ident_f[:], [XB, B("ident_f")], [PBK[bk]], sig=(i == 3))
                c0 = (q4 * W4 + q * 512) // 128
                eng = "act" if q % 2 else "dve"
                cp(eng, xT[:, c0:c0 + 4, j * 128:(j + 1) * 128],
                   pbank[bk][:, 0:512].rearrange("p (c t) -> p c t", c=4), [PBK[bk]], [B("xT")])

    pj = {"bank": 0}

    hook = {"nxt": None}

    def mini():
        if hook["nxt"] is not None:
            hook["nxt"].load_one()
        advance(1)

    def proj_tiles(wslot, nkc, ncols, ntiles, evac, tick, lhs=None, lhs_buf=None):
        lhs = xT if lhs is None else lhs
        lhs_buf = [B("xT")] if lhs_buf is None else lhs_buf
        pend = None
        for j in range(ntiles):
            bk = pj["bank"] % 3
            pj["bank"] += 1
            for kc in range(nkc):
                mm(pbank[bk][:, 0:ncols], lhs[:, kc, j * 128:(j + 1) * 128], wbf[wslot][:, kc, 0:ncols],
                   kc == 0, kc == nkc - 1, lhs_buf + [wb(wslot, kc)], [PBK[bk]], sig=(kc == nkc - 1))
                if kc % 8 == 7 and kc != nkc - 1:
                    mini()
                if kc == 7 and pend is not None:
                    evac(pend[0], pbank[pend[1]], PBK[pend[1]])
                    pend = None
            if pend is not None:
                evac(pend[0], pbank[pend[1]], PBK[pend[1]])
            pend = (j, bk)
            tick()
        evac(pend[0], pbank[pend[1]], PBK[pend[1]])

    def gates_math(tiles):
        nt = len(tiles)
        GB = [B("gates")]
        SCR = bcsrc[:, 0:128]
        for j in range(nt):
            tt("dve", gl[:, j, :], gl[:, j, :], bif[:], ALU.add, [B("bif")] + GB, GB)
        zf = gl[:, 0:nt, NH:2 * NH]
        tmpa = sml[:, 0:nt * NH].rearrange("p (j h) -> p j h", j=nt)
        stt(tmpa, zf, -1.0, zf, ALU.mult, ALU.max, GB, [B("sml")])
        act(tmpa, tmpa, AF.Exp, [B("sml")], [B("sml")], scale=-1.0)
        act(tmpa, tmpa, AF.Ln, [B("sml")], [B("sml")], bias=1.0)
        stt(zf, zf, 0.0, tmpa, ALU.min, ALU.subtract, [B("sml")] + GB, GB)
        for j in range(nt):
            tr(pbank[3][0:NH, j * 128:(j + 1) * 128], gl[:, j, 0:NH], ident_f[:], GB + CONST, [PBK[3]], sig=(j == nt - 1))
        for j in range(nt):
            tr(pbank[4][0:NH, j * 128:(j + 1) * 128], gl[:, j, NH:2 * NH], ident_f[:], GB + CONST, [PBK[4]], sig=(j == nt - 1))
        cp("dve", gA[:, 0:nt * 128], pbank[3][0:NH, 0:nt * 128], [PBK[3]], GB)
        cp("dve", gBm[:, 0:nt * 128], pbank[4][0:NH, 0:nt * 128], [PBK[4]], GB)
        np_ = sum(1 for k, _ in tiles if k in ("p", "x"))
        has_s = any(k == "s" for k, _ in tiles)
        if np_:
            W = np_ * 128
            E("dve", "memset", [], GB, gW[:, 0:W], 0.0)
            E("dve", "tensor_tensor_scan", GB + [B("gsm")], GB, out=gC[:, 0:W], data0=gBm[:, 0:W], data1=gW[:, 0:W],
              initial=gsm[:, 0:1], op0=ALU.add, op1=ALU.add)
            E("dve", "tensor_tensor_scan", GB + [B("gsm")], GB, out=gD[:, 0:W], data0=gBm[:, 0:W], data1=gA[:, 0:W],
              initial=gsm[:, 1:2], op0=ALU.add, op1=ALU.max)
        if has_s:
            o = np_ * 128
            dma(sgA[:], sm_d.rearrange("b h -> h b"), writes=[B("sgA")], allow_slow_non_contiguous=True)
            lf3 = gBm[:, o:o + 128].rearrange("h (s t) -> h s t", t=8)
            ig3 = gA[:, o:o + 128].rearrange("h (s t) -> h s t", t=8)
            tt("dve", sgB[:], lf3[:, :, 0], sgA[:], ALU.add, GB + [B("sgA")], [B("sgB")])
            E("dve", "tensor_tensor_scan", GB + [B("rst")], GB, out=gC[:, o:o + 128], data0=rst[:, 0, :],
              data1=gBm[:, o:o + 128], initial=0.0, op0=ALU.mult, op1=ALU.add)
            tt("dve", SCR, gBm[:, o:o + 128], rst[:, 1, :], ALU.add, GB + [B("rst")], [B("bcsrc")])
            cp("dve", gW[:, 0:128], gA[:, o:o + 128], GB, GB)
            gw3 = gW[:, 0:128].rearrange("h (s t) -> h s t", t=8)
            tt("dve", gw3[:, :, 0], ig3[:, :, 0], sgB[:], ALU.max, GB + [B("sgB")], GB)
            E("dve", "tensor_tensor_scan", GB + [B("bcsrc")], GB, out=gD[:, o:o + 128], data0=SCR, data1=gW[:, 0:128],
              initial=0.0, op0=ALU.add, op1=ALU.max)
            cp("dve", smo[:], gD[:, o:o + 128].rearrange("h (s t) -> h s t", t=8)[:, :, 7], GB, [B("smo")])
        W = nt * 128
        tt("dve", gA[:, 0:W], gC[:, 0:W], gA[:, 0:W], ALU.subtract, GB, GB)
        if np_:
            Wp = np_ * 128
            tt("dve", gsm[:, 2:3], gsm[:, 0:1], gsm[:, 1:2], ALU.subtract, [B("gsm")], [B("gsm")])
            cp("dve", gsm[:, 4:5], gC[:, Wp - 1:Wp], GB, [B("gsm")])
            cp("dve", gsm[:, 5:6], gD[:, Wp - 1:Wp], GB, [B("gsm")])
        tt("dve", gC[:, 0:W], gC[:, 0:W], gD[:, 0:W], ALU.subtract, GB, GB)
        act(gD[:, 0:W], gD[:, 0:W], AF.Exp, GB, GB, scale=-1.0)
        for j, (kind, _) in enumerate(tiles):
            sl = slice(j * 128, (j + 1) * 128)
            if kind in ("p", "x"):
                c = 8 + 4 * j
                if j == 0:
                    cp("dve", gsm[:, c:c + 1], gsm[:, 2:3], [B("gsm")], [B("gsm")])
                else:
                    cp("dve", gsm[:, c:c + 1], gC[:, j * 128 - 1:j * 128], GB, [B("gsm")])
                cp("dve", gsm[:, c + 1:c + 2], gC[:, (j + 1) * 128 - 1:(j + 1) * 128], GB, [B("gsm")])
                ts("dve", gsm[:, c + 3:c + 4], gsm[:, c:c + 1], -1.0, ALU.mult, [B("gsm")], [B("gsm")])
                act(gBm[:, sl], gC[:, sl], AF.Exp, GB + [B("gsm")], GB, bias=gsm[:, c + 3:c + 4])
                act(gW[:, sl], gA[:, sl], AF.Exp, GB + [B("gsm")], GB, scale=-1.0, bias=gsm[:, c + 1:c + 2])
                act(gsm[:, c + 2:c + 3], gsm[:, c + 1:c + 2], AF.Exp, [B("gsm")], [B("gsm")], bias=gsm[:, c + 3:c + 4])
            else:
                u3 = gC[:, sl].rearrange("h (s t) -> h s t", t=8)
                g3 = gA[:, sl].rearrange("h (s t) -> h s t", t=8)
                s3 = SCR.rearrange("h (s t) -> h s t", t=8)
                tt("dve", s3, u3, sgA[:].unsqueeze(2).to_broadcast([NH, 16, 8]), ALU.add, GB + [B("sgA")], [B("bcsrc")])
                act(gBm[:, sl], SCR, AF.Exp, [B("bcsrc")], GB)
                tt("dve", s3, u3[:, :, 7:8].to_broadcast([NH, 16, 8]), g3, ALU.subtract, GB, [B("bcsrc")])
                act(gW[:, sl], SCR, AF.Exp, [B("bcsrc")], GB)
                tt("dve", sgB[:], u3[:, :, 7], sgA[:], ALU.add, GB + [B("sgA")], [B("sgB")])
                act(sgB[:], sgB[:], AF.Exp, [B("sgB")], [B("sgB")])
        for j in range(nt):
            sl = slice(j * 128, (j + 1) * 128)
            for qi, src in enumerate((gA, gBm, gD, gW)):
                tr(pbank[3][:, (j * 4 + qi) * NH:(j * 4 + qi + 1) * NH], src[:, sl], ident_f[0:NH, 0:NH],
                   GB + CONST, [PBK[3]], sig=(qi == 3 and j == nt - 1))
        cp("dve", gtm[:, 0:nt].rearrange("p j q h -> p (j q h)"), pbank[3][:, 0:nt * 4 * NH], [PBK[3]], [B("gtm")])
        for j in range(nt):
            ts("dve", gtm[:, j, 0, :], gtm[:, j, 0, :], -1.0, ALU.mult, [B("gtm")], [B("gtm")])
        if np_:
            cp("dve", gsm[:, 0:1], gsm[:, 4:5], [B("gsm")], [B("gsm")])
            cp("dve", gsm[:, 1:2], gsm[:, 5:6], [B("gsm")], [B("gsm")])

    def mlstm_chunk(kind, j, h, par, tile_idx):
        v = phv[par]
        PHB = [B(f"ph{par}")]
        GB = [B("gates"), B("gtm"), B("gsm")]
        tok = slice(j * 128, (j + 1) * 128)
        MT = [B("mtmp")]
        nd = 16 if kind == "s" else 1
        cp("dve", bcsrc[:, 0:128], gC[:, tok], GB, [B("bcsrc")])
        if kind == "s":
            cp("dve", bcsrc[:, 128:144], sgB[:], [B("sgB")], [B("bcsrc")])
        else:
            cp("dve", bcsrc[:, 128:129], gsm[:, 8 + 4 * j + 2:8 + 4 * j + 3], GB, [B("bcsrc")])
        ts("dve", bcsrc[:, 0:128 + nd], bcsrc[:, 0:128 + nd], ident_f[0:NH, h:h + 1], ALU.mult,
           [B("bcsrc")] + CONST, [B("bcsrc")])
        if kind in ("p", "x"):
            ts("dve", kw[:], v["km"][:, j, :], gtm[:, j, 3, h:h + 1], ALU.mult, PHB + GB, [B("kw")])
        yield
        mm(pbank[3][:, 0:128 + nd], ones_nh[:], bcsrc[:, 0:128 + nd], True, True, [B("ones_nh"), B("bcsrc")], [PBK[3]])
        cp("dve", bcs[:, 0:nd], pbank[3][:, 128:128 + nd], [PBK[3]], [B("bcs")])
        if kind != "x":
            mi = 1 if kind == "s" else 0
            tt("dve", DTp[:], pbank[3][:, 0:128], dmask[:, mi, :], ALU.add, [PBK[3], B("dmask")], MT)
            act(DTe[:], DTp[:], AF.Exp, MT + GB, MT, bias=gtm[:, j, 0, h:h + 1])
            yield
            for dc in range(2):
                mm(pbank[4][:, 0:128], v["kT"][:, dc, tok], v["qT"][:, dc, tok], dc == 0, dc == 1, PHB, [PBK[4]],
                   sig=(dc == 1))
            tt("dve", SDT[:], pbank[4][:, 0:128], DTe[:], ALU.mult, [PBK[4]] + MT, [B("SDT")])
            yield
            mm(pbank[4][:, 0:257], SDT[:], v["vaug"][:, j, :], True, True, [B("SDT")] + PHB, [PBK[4]])
            if kind == "p":
                cb = cbf[h % 2]
                for dc in range(2):
                    mm(pbank[5][:, 0:257], v["qT"][:, dc, tok], cb[:, dc, :], dc == 0, dc == 1,
                       PHB + [B(f"cbf{h % 2}")], [PBK[5]], sig=(dc == 1))
            cp("act", P1s[:], pbank[4][:, 0:257], [PBK[4]], [B("P1s")])
            yield
        if kind == "s":
            it = 0
            dma(hot[0:16, :], sn_d[:, h, :], writes=[B("hot")])
            for dc in range(2):
                tr(pbank[3][:, 320 + dc * 16:336 + dc * 16], hot[0:16, dc * 128:(dc + 1) * 128], ident_f[0:16, 0:16],
                   [B("hot")] + CONST, [PBK[3]], sig=(dc == 1))
            cp("dve", nall[:].rearrange("p c s -> p (c s)"), pbank[3][:, 320:352], [PBK[3]], [B("nall")])
            ts("dve", wm16[:], seqm[:], gtm[:, j, 3, h:h + 1], ALU.mult, GB + [B("seqm")], [B("wm16")])
            PRE = 3

            def issue_load(k):
                dc_, s_ = divmod(k, 16)
                c_ = k % NCS
                dma(csf[c_][:, 0:256], sC_d[s_, h, dc_ * 128:(dc_ + 1) * 128, :], writes=[B(f"csf{c_}")])

            for k in range(PRE):
                issue_load(k)
            for dc in range(2):
                for s in range(16):
                    sl_ = it % 2
                    c4 = it % NCS
                    if it + PRE < 32:
                        issue_load(it + PRE)
                    it += 1
                    CF = [B(f"csf{c4}")]
                    CN = [B(f"csn{c4}")]
                    cp("dve", csf[c4][:, 256:257], nall[:, dc, s:s + 1], [B("nall")], CN)
                    tt("dve", mqf[:, sl_, :], v["qT"][:, dc, tok], colm[:, s, :], ALU.mult, PHB + [B("colm")], [B(f"mq{sl_}")])
                    ts("dve", kw2[:, sl_, :], v["km"][:, j, dc * 128:(dc + 1) * 128], wm16[:, s:s + 1], ALU.mult,
                       PHB + [B("wm16")], [B(f"kw2{sl_}")])
                    yield
                    mm(pbank[5][:, 0:257], mqf[:, sl_, :], csf[c4][:], dc == 0 and s == 0, dc == 1 and s == 15,
                       [B(f"mq{sl_}")] + CF + CN, [PBK[5]], sig=(dc == 1 and s == 15))
                    mm(pbank[3][:, 0:257], kw2[:, sl_, :], v["vaug"][:, j, :], True, True, [B(f"kw2{sl_}")] + PHB, [PBK[3]])
                    stt(csf[c4][:], csf[c4][:], bcs[:, s:s + 1], pbank[3][:, 0:257], ALU.mult, ALU.add,
                        [PBK[3], B("bcs")], CF + CN)
                    cp("dve", nout[:, dc, s:s + 1], csf[c4][:, 256:257], CN, [B("nout")])
                    dma(oC_d[s, h, dc * 128:(dc + 1) * 128, :], csf[c4][:, 0:256], reads=CF)
            for dc in range(2):
                tr(pbank[3][0:16, 256 * 0 + dc * 128:(dc + 1) * 128], nout[:, dc, :], ident_f[:], [B("nout")] + CONST, [PBK[3]],
                   sig=(dc == 1))
            cp("dve", hot[0:16, :], pbank[3][0:16, 0:256], [PBK[3]], [B("hot")])
            dma(on_d[:, h, :], hot[0:16, :], reads=[B("hot")])
            yield
        if kind != "x":
            stt(ndt[:], pbank[5][:, 0:257], gtm[:, j, 1, h:h + 1], P1s[:], ALU.mult, ALU.add,
                [PBK[5], B("P1s")] + GB, [B("ndt")])
            stt(sml[:, 5:6], ndt[:, 256:257], -1.0, ndt[:, 256:257], ALU.mult, ALU.max, [B("ndt")], [B("sml")])
            ts("dve", sml[:, 0:1], sml[:, 5:6], gtm[:, j, 2, h:h + 1], ALU.max, [B("sml")] + GB, [B("sml")])
            em.op("dve", lambda e: e.reciprocal(out=sml[:, 1:2], in_=sml[:, 0:1]), [B("sml")], [B("sml")])
            stt(hot[:], ndt[:, 0:256], sml[:, 1:2], v["sgo"][:, j, :], ALU.mult, ALU.mult, [B("ndt"), B("sml")] + PHB,
                [B("hot")])
            em.op("dve", lambda e: e.bn_stats(out=stats[:, 0, :], in_=hot[:]), [B("hot")], [B("stats")])
            em.op("dve", lambda e: e.bn_aggr(out=sml[:, 2:4], in_=stats[:, 0, :]), [B("stats")], [B("sml")])
            act(sml[:, 4:5], sml[:, 3:4], AF.Ln, [B("sml")], [B("sml")], bias=LN_EPS)
            act(sml[:, 4:5], sml[:, 4:5], AF.Exp, [B("sml")], [B("sml")], scale=-0.5)
            ts("dve", tmpm[:], v["gz"][:, j, :], sml[:, 4:5], ALU.mult, PHB + [B("sml")], [B("tmpm")])
            stt(mixtm[:], hot[:], sml[:, 2:3], tmpm[:], ALU.subtract, ALU.mult, [B("hot"), B("sml"), B("tmpm")],
                [B("mixtm")])
            yield
            yield
            for dc in range(2):
                tr(pbf16[:, 512 + dc * 128:512 + (dc + 1) * 128], mixtm[:, dc * 128:(dc + 1) * 128], ident_b[:],
                   [B("mixtm")] + CONST, [PBK[7]], sig=(dc == 1))
            cp("act", mixT[:, 2 * h:2 * h + 2, tok], pbf16[:, 512:768].rearrange("p (c t) -> p c t", c=2), [PBK[7]],
               [B("mixT")])
            yield
        if kind in ("p", "x"):
            cub = (5, 3)
            for dc in range(2):
                mm(pbank[cub[dc]][:, 0:257], kw[:, dc * 128:(dc + 1) * 128], v["vaug"][:, j, :], True, True, [B("kw")] + PHB,
                   [PBK[cub[dc]]])
            yield
            for dc in range(2):
                stt(cst[:, h, dc, :], cst[:, h, dc, :], bcs[:, 0:1], pbank[cub[dc]][:, 0:257], ALU.mult, ALU.add,
                    [PBK[cub[dc]], B("bcs")], [B(f"cst{h}")])
            yield
            if kind == "p":
                cp("act", cbf[h % 2][:], cst[:, h, :, :], [B(f"cst{h}")], [B(f"cbf{h % 2}")])
        yield

    def attn_units(kind, j, h, par, first_tile):
        v = phv[par]
        PHB = [B(f"ph{par}")]
        tok = slice(j * 128, (j + 1) * 128)
        AT = [B("atmp")]
        mi = 2 if kind == "s" else (1 if first_tile else 0)
        if kind == "s":
            cstb = cstg[:, 256:384].bitcast(BF16)
            for s2 in range(8):
                dma(cstg[:, 0:128].rearrange("p (s c) -> p s c", s=2),
                    ck_d[2 * s2:2 * s2 + 2, :, h * 64:(h + 1) * 64].rearrange("s k c -> k s c"), writes=[B("cstg")])
                cp("dve", cstb[:, 0:128], cstg[:, 0:128], [B("cstg")], [B("cstgb")])
                for i in range(2):
                    tr(pbf16[0:64, 768 + i * 128:768 + (i + 1) * 128], cstb[:, i * 64:(i + 1) * 64], ident_b[:],
                       [B("cstgb")] + CONST, [PBK[7]], sig=(i == 1))
                cp("act", KcT[0:64, 2 * s2:2 * s2 + 2, :], pbf16[0:64, 768:1024].rearrange("p (s t) -> p s t", s=2),
                   [PBK[7]], [B("KcT")])
                dma(cstg[:, 128:256].rearrange("p (s c) -> p s c", s=2),
                    cv_d[2 * s2:2 * s2 + 2, :, h * 64:(h + 1) * 64].rearrange("s k c -> k s c"), writes=[B("cstgv")])
                cp("dve", Vc[:, 2 * s2:2 * s2 + 2, :], cstg[:, 128:256].rearrange("p (s c) -> p s c", s=2),
                   [B("cstgv")], [B("Vc")])
                yield
            yield
        for g in range(4):
            qT_g = v["qaT"][0:64, g, tok]
            if kind == "s":
                tt("dve", mskb[0:64], qT_g.unsqueeze(1).to_broadcast([64, 16, 128]), colm[0:64], ALU.mult,
                   PHB + [B("colm")], [B("mskb")])
                for s in range(16):
                    mm(pbank[6][:, 0:128], mskb[0:64, s, :], KcT[0:64, s, :], s == 0, s == 15, [B("mskb"), B("KcT")],
                       [PBK[6]], sig=False)
            else:
                kprev = kaT_c[0:64, h, :] if j == 0 else v["kaT"][0:64, j - 1, :]
                mm(pbank[6][:, 0:128], qT_g, kprev, True, True, PHB + [B(f"kaTc{h}")], [PBK[6]], sig=False)
            mm(pbank[6][:, 128:256], qT_g, v["kaT"][0:64, j, :], True, True, PHB, [PBK[6]])
            stt(sc[:], pbank[6][:, 0:256], 0.125, amask[:, mi, :], ALU.mult, ALU.add, [PBK[6], B("amask")], AT)
            em.op("dve", lambda e: e.reduce_max(out=asml[:, 0:1], in_=sc[:], axis=AX.X), AT, [B("asml")])
            si = h * 4 + g
            ts("dve", asml[:, 1:2], asml[:, 0:1], sinks[:, si:si + 1], ALU.max, [B("asml"), B("sinks")], [B("asml")],
               s2=-1.0, op1=ALU.mult)
            act(pbf[:], sc[:], AF.Exp, AT + [B("asml")], [B("pbf"), B("asml2")], bias=asml[:, 1:2], accum=asml[:, 4:5])
            act(asml[:, 2:3], sinks[:, si:si + 1], AF.Exp, [B("sinks"), B("asml")], [B("asml3")], bias=asml[:, 1:2])
            tt("dve", asml[:, 3:4], asml[:, 2:3], asml[:, 4:5], ALU.add, [B("asml2"), B("asml3")], [B("asml4")])
            em.op("dve", lambda e: e.reciprocal(out=asml[:, 5:6], in_=asml[:, 3:4]), [B("asml4")], [B("asml4")])
            yield
            for c in range(2):
                tr(pbf16[:, 768 + c * 128:768 + (c + 1) * 128], pbf[:, c * 128:(c + 1) * 128], ident_b[:],
                   [B("pbf")] + CONST, [PBK[7]], sig=(c == 1))
            cp("act", pT[:], pbf16[:, 768:1024].rearrange("p (c t) -> p c t", c=2), [PBK[7]], [B("pT")])
            yield
            if kind == "s":
                tt("dve", mskb[:], pT[:, 0, :].unsqueeze(1).to_broadcast([128, 16, 128]), colm[:], ALU.mult,
                   [B("pT"), B("colm")], [B("mskb")])
                for s in range(16):
                    mm(pbank[6][:, 256:320], mskb[:, s, :], Vc[:, s, :], s == 0, False, [B("mskb"), B("Vc")], [PBK[6]],
                       sig=False)
            else:
                vprev = va_c[:, h, :] if j == 0 else v["va"][:, j - 1, :]
                mm(pbank[6][:, 256:320], pT[:, 0, :], vprev, True, False, [B("pT"), B(f"vac{h}")] + PHB, [PBK[6]], sig=False)
            mm(pbank[6][:, 256:320], pT[:, 1, :], v["va"][:, j, :], False, True, [B("pT")] + PHB, [PBK[6]])
            stt(mixa[:, g * 64:(g + 1) * 64], pbank[6][:, 256:320], asml[:, 5:6], v["za"][:, j, g * 64:(g + 1) * 64],
                ALU.mult, ALU.mult, [PBK[6], B("asml4")] + PHB, [B("mixa")])
            yield
        for dc in range(2):
            tr(pbf16[:, 768 + dc * 128:768 + (dc + 1) * 128], mixa[:, dc * 128:(dc + 1) * 128], ident_b[:],
               [B("mixa")] + CONST, [PBK[7]], sig=(dc == 1))
        cp("act", mixT[:, KCM + 2 * h:KCM + 2 * h + 2, tok], pbf16[:, 768:1024].rearrange("p (c t) -> p c t", c=2),
           [PBK[7]], [B("mixT")])
        yield

    def make_head_tasks(tiles, h, mode):
        par = h % 2
        v = phv[par]
        PHB = [B(f"ph{par}")]
        nt = len(tiles)
        out = []

        def wcols(c0, n):
            return wpiece("w_in", c0, n, KC)

        def ev_q(j, ps, pb):
            cp("act", evb[:], ps[:, 0:256], [pb], [B("evb")])
            for dc in range(2):
                tr(pbf16[:, dc * 128:(dc + 1) * 128], evb[:, dc * 128:(dc + 1) * 128], ident_b[:], [B("evb")] + CONST,
                   [PBK[7]], sig=(dc == 1))
            cp("act", v["qT"][:, :, j * 128:(j + 1) * 128], pbf16[:, 0:256].rearrange("p (c t) -> p c t", c=2), [PBK[7]], PHB)

        def ev_k(j, ps, pb):
            act(v["km"][:, j, :], ps[:, 0:256], AF.Copy, [pb], PHB, scale=1.0 / 16.0)
            if mode == "main":
                for dc in range(2):
                    tr(pbf16[:, 256 + dc * 128:256 + (dc + 1) * 128], v["km"][:, j, dc * 128:(dc + 1) * 128], ident_b[:],
                       PHB + CONST, [PBK[7]], sig=(dc == 1))
                cp("act", v["kT"][:, :, j * 128:(j + 1) * 128], pbf16[:, 256:512].rearrange("p (c t) -> p c t", c=2),
                   [PBK[7]], PHB)

        def ev_v(j, ps, pb):
            cp("act", v["vaug"][:, j, 0:256], ps[:, 0:256], [pb], PHB)

        def ev_o(j, ps, pb):
            act(v["sgo"][:, j, :], ps[:, 0:256], AF.Sigmoid, [pb], PHB)

        def ev_z(j, ps, pb):
            act(ztmp[:], ps[:, 0:256], AF.Silu, [pb], [B("ztmp")])
            tt("dve", v["gz"][:, j, :], ztmp[:], ngh[par][:], ALU.mult, [B("ztmp"), B(f"ngh{par}")], PHB)

        def rope_apply(dst, src, ng, ti, rd, wr):
            cosb = rope[:, ti, 0, :].unsqueeze(1).to_broadcast([128, ng, 64])
            t3 = ropet[:, 0:ng * 64].rearrange("p (g e) -> p g e", g=ng)
            tt("dve", t3, src, cosb, ALU.mult, rd + [B("rope")], [B("ropet")])
            sn1 = rope[:, ti, 1, 0:32].unsqueeze(1).to_broadcast([128, ng, 32])
            sn2 = rope[:, ti, 1, 32:64].unsqueeze(1).to_broadcast([128, ng, 32])
            u3 = ropet[:, 256:256 + ng * 64].rearrange("p (g e) -> p g e", g=ng)
            tt("dve", u3[:, :, 0:32], src[:, :, 32:64], sn1, ALU.mult, rd + [B("rope")], [B("ropeu")])
            tt("dve", u3[:, :, 32:64], src[:, :, 0:32], sn2, ALU.mult, rd + [B("rope")], [B("ropeu")])
            tt("dve", dst, t3, u3, ALU.add, [B("ropet"), B("ropeu")], wr)

        def tile_rope_idx(j):
            return j

        def ev_qa(j, ps, pb):
            rope_apply(qrot[:].rearrange("p (g e) -> p g e", g=4), ps[:, 0:256].rearrange("p (g e) -> p g e", g=4), 4,
                       tile_rope_idx(j), [pb], [B("qrot")])
            for g in range(4):
                tr(pbf16[0:64, g * 128:(g + 1) * 128], qrot[:, g * 64:(g + 1) * 64], ident_b[:], [B("qrot")] + CONST,
                   [PBK[7]], sig=(g == 3))
            cp("act", v["qaT"][0:64, :, j * 128:(j + 1) * 128], pbf16[0:64, 0:512].rearrange("p (g t) -> p g t", g=4),
               [PBK[7]], PHB)

        def ev_kv(j, ps, pb):
            kind, idx = tiles[j]
            cp("act", kvf[:], ps[:, 0:128], [pb], [B("kvf")])
            rope_apply(krot[:].unsqueeze(1), kvf[:, 0:64].unsqueeze(1), 1, tile_rope_idx(j), [B("kvf")], [B("krot")])
            cp("dve", krb[:], krot[:], [B("krot")], [B("krb")])
            tr(pbf16[0:64, 0:128], krb[:], ident_b[:], [B("krb")] + CONST, [PBK[7]])
            if mode == "pre":
                cp("dve", kaT_c[0:64, h, :], pbf16[0:64, 0:128], [PBK[7]], [B(f"kaTc{h}")])
                cp("act", va_c[:, h, :], kvf[:, 64:128], [B("kvf")], [B(f"vac{h}")])
                return
            cp("act", v["kaT"][0:64, j, :], pbf16[0:64, 0:128], [PBK[7]], PHB)
            cp("act", v["va"][:, j, :], kvf[:, 64:128], [B("kvf")], PHB)
            if kind == "p" and idx == NPT - 1:
                dma(pk_d[:, h * 64:(h + 1) * 64], krot[:], reads=[B("krot")])
                dma(pv_d[:, h * 64:(h + 1) * 64], kvf[:, 64:128], reads=[B("kvf")])
            if kind == "s":
                dma(ok_d[:, 120:128, h * 64:(h + 1) * 64], krot[:], reads=[B("krot")])
                dma(ov_d[:, 120:128, h * 64:(h + 1) * 64], kvf[:, 64:128], reads=[B("kvf")])

        def ev_za(j, ps, pb):
            act(v["za"][:, j, :], ps[:, 0:256], AF.Silu, [pb], PHB)

        def mk(c0, n, ev, pre=None):
            def run(wslot, tick):
                if pre:
                    pre()
                proj_tiles(wslot, KC, n, nt, ev, tick)
            tk = Task([(wcols(c0, n), KC, n, 0)], run)
            tk.nticks = nt
            tk.nhooks = nt * (KC // 8 - 1)
            return tk

        def run_kv(wslot, tick):
            if mode == "pre":
                lastj = nt - 1
                bk = pj["bank"] % 3
                pj["bank"] += 1
                for kc in range(KC):
                    mm(pbank[bk][:, 0:128], xT[:, kc, lastj * 128:(lastj + 1) * 128], wbf[wslot][:, kc, 0:128], kc == 0,
                       kc == KC - 1, [B("xT"), wb(wslot, kc)], [PBK[bk]], sig=(kc == KC - 1))
                ev_kv(lastj, pbank[bk], PBK[bk])
                tick()
            else:
                proj_tiles(wslot, KC, 128, nt, ev_kv, tick)

        def load_ng():
            dma(ngh[par][:], ng_d[:, h * 256:(h + 1) * 256], writes=[B(f"ngh{par}")])

        if mode == "main":
            out.append(mk(cfg["c_qm"] + h * 256, 256, ev_q, pre=load_ng))
        out.append(mk(cfg["c_km"] + h * 256, 256, ev_k))
        out.append(mk(cfg["c_vm"] + h * 256, 256, ev_v))
        if mode == "main":
            out.append(mk(cfg["c_om"] + h * 256, 256, ev_o))
            out.append(mk(cfg["c_zm"] + h * 256, 256, ev_z))
            out.append(mk(cfg["c_qa"] + h * 256, 256, ev_qa))
        kvt = Task([(wcols(cfg["c_ka"] + h * 64, 64), KC, 64, 0), (wcols(cfg["c_va"] + h * 64, 64), KC, 64, 64)], run_kv)
        kvt.nticks = 1 if mode == "pre" else nt
        kvt.nhooks = 0 if mode == "pre" else nt * (KC // 8 - 1)
        if mode == "main" or tiles[-1] == ("x", NPT - 1):
            out.append(kvt)
        if mode == "main":
            out.append(mk(cfg["c_za"] + h * 256, 256, ev_za))
        return out

    def group_tasks(tiles, mode, gi):
        nt = len(tiles)
        tl = []

        def setup(wslot, tick):
            drain()
            for par in range(2):
                em.op("dve", lambda e, par=par: e.memset(phv[par]["vaug"][:, :, 256:257], 1.0), [], [B(f"ph{par}")])
            for j, (kind, idx) in enumerate(tiles):
                src = xp_d[idx] if kind == "x" else (x_d[NPT] if kind == "s" else x_d[idx])
                load_x_tile(src, j)
                ri = NPT if kind == "s" else (NT if kind == "x" else idx)
                dma(rope[:, j, :, :], rope_d[:, ri, :, :], writes=[B("rope")])

            def ev_g(j, ps, pb):
                cp("dve", gl[:, j, :], ps[:, 0:2 * NH], [pb], [B("gates")])
            proj_tiles(wslot, KC, 2 * NH, nt, ev_g, tick)
            gates_math(tiles)

        ts_ = Task([(wpiece("w_in", cfg["c_i"], 2 * NH, KC), KC, 2 * NH, 0)], setup)
        ts_.nticks = nt
        ts_.nhooks = nt * (KC // 8 - 1)
        tl.append(ts_)

        for h in range(NH):
            ht = make_head_tasks(tiles, h, mode)
            last = ht[-1]

            def wrap(run, h=h):
                def run2(wslot, tick):
                    run(wslot, tick)
                    drain()
                    par = h % 2

                    def mixer_gen():
                        if mode == "main":
                            cp("act", cbf[h % 2][:], cst[:, h, :, :], [B(f"cst{h}")], [B(f"cbf{h % 2}")])
                        for j, (kind, idx) in enumerate(tiles):
                            yield from mlstm_chunk(kind, j, h, par, idx)

                    def attn_gen():
                        for j, (kind, idx) in enumerate(tiles):
                            yield from attn_units(kind, j, h, par, kind == "p" and idx == 0)
                        lastp = [j for j, (k, _) in enumerate(tiles) if k == "p"]
                        if lastp:
                            jl = lastp[-1]
                            v = phv[par]
                            cp("dve", kaT_c[0:64, h, :], v["kaT"][0:64, jl, :], [B(f"ph{par}")], [B(f"kaTc{h}")])
                            cp("dve", va_c[:, h, :], v["va"][:, jl, :], [B(f"ph{par}")], [B(f"vac{h}")])
                        yield

                    gens.append(mixer_gen())
                    if mode == "main":
                        gens.append(attn_gen())
                return run2
            last.run = wrap(last.run)
            tl.extend(ht)

        if mode == "pre":
            return tl

        T = nt * 128
        for cb in range(KC):
            def run_m(wslot_a, tick, cb=cb):
                pass
            def mk_half(cb, half):
                gcol = cfg["c_gm"] if half == 0 else cfg["c_ga"]
                wsrc = "w_bm" if half == 0 else "w_ba"
                pieces = [(wpiece("w_in", gcol + cb * 128, 128, KC), KC, 128, 0),
                          (wpiece(wsrc, cb * 128, 128, KCM), KCM, 128, 128)]
                bg, bb = (3, 5) if half == 0 else (4, 6)

                def run_h(wslot, tick):
                    if cb == 0 and half == 0:
                        drain()
                    for kc in range(KC):
                        mm(pbank[bg][:, 0:T], wbf[wslot][:, kc, 0:128], xT[:, kc, 0:T], kc == 0,
                           kc == KC - 1, [B("xT"), wb(wslot, kc)], [PBK[bg]], sig=(kc == KC - 1))
                        if kc % 8 == 7 and kc != KC - 1:
                            mini()
                    act(misc[:, half * 512:half * 512 + T], pbank[bg][:, 0:T], AF.Sigmoid, [PBK[bg]], [B(f"msig{half}")])
                    tick()
                    for kc in range(KCM):
                        mm(pbank[bb][:, 0:T], wbf[wslot][:, kc, 128:256], mixT[:, half * KCM + kc, 0:T],
                           kc == 0, kc == KCM - 1, [B("mixT"), wb(wslot, kc)], [PBK[bb]], sig=(kc == KCM - 1))
                        if kc % 8 == 7 and kc != KCM - 1:
                            mini()
                    tt("dve", misc[:, 1024 + half * 512:1024 + half * 512 + T], pbank[bb][:, 0:T],
                       misc[:, half * 512:half * 512 + T], ALU.mult, [PBK[bb], B(f"msig{half}")], [B(f"mprod{half}")])
                    tick()
                    if half == 1:
                        wr = [B("mergedT")] + ([B("ph0"), B("ph1")] if cb == 0 else [])
                        tt("dve", mergedT[:, cb, 0:T], misc[:, 1024:1024 + T], misc[:, 1536:1536 + T], ALU.add,
                           [B("mprod0"), B("mprod1")], wr)
                tk = Task(pieces, run_h)
                tk.nticks = 2
                tk.nhooks = (KC // 8 - 1) + max(0, KCM // 8 - 1)
                return tk
            tl.append(mk_half(cb, 0))
            tl.append(mk_half(cb, 1))

        NOB = D // 256
        for ob in range(NOB):
            def run_o(wslot, tick, ob=ob):
                if ob == 0:
                    for j, (kind, idx) in enumerate(tiles):
                        src = x_d[NPT] if kind == "s" else x_d[idx]
                        dma(ypre[:, j, :], src, writes=[B(f"ypre{j}"), B("xT"), B("mixT")])

                def ev(j, ps, pb):
                    stt(ypre[:, j, ob * 256:(ob + 1) * 256], ypre[:, j, ob * 256:(ob + 1) * 256], alpha, ps[:, 0:256],
                        ALU.mult, ALU.add, [pb], [B(f"ypre{j}")])
                proj_tiles(wslot, KC, 256, nt, ev, tick, lhs=mergedT, lhs_buf=[B("mergedT"), B("ph0"), B("ph1")])
            to_ = Task([(wpiece("w_out", ob * 256, 256, KC), KC, 256, 0)], run_o)
            to_.nticks = nt
            to_.nhooks = nt * (KC // 8 - 1)
            tl.append(to_)

        def run_ln(wslot, tick):
            nch = max(1, D // 512)
            cw = D // nch
            for j, (kind, idx) in enumerate(tiles):
                YB = [B(f"ypre{j}")]
                for c in range(nch):
                    E("dve", "bn_stats", YB, [B("stats")], out=stats[:, c, :], in_=ypre[:, j, c * cw:(c + 1) * cw])
                em.op("dve", lambda e: e.bn_aggr(out=sml[:, 8:10], in_=stats[:, 0:nch, :].rearrange("p c s -> p (c s)")),
                      [B("stats")], [B("sml")])
                act(sml[:, 10:11], sml[:, 9:10], AF.Ln, [B("sml")], [B("sml")], bias=LN_EPS)
                act(sml[:, 10:11], sml[:, 10:11], AF.Exp, [B("sml")], [B("sml")], scale=-0.5)
                for c0 in range(0, D, 1024):
                    w = min(1024, D - c0)
                    MW_ = [B("misc"), B("msig0"), B("msig1"), B("mprod0"), B("mprod1"), B("miscx0"), B("miscx1")]
                    dma(misc[:, 0:w], lng_d[:, c0:c0 + w], writes=MW_)
                    dma(misc[:, 1024:1024 + w], lnb_d[:, c0:c0 + w], writes=MW_)
                    stt(ypre[:, j, c0:c0 + w], ypre[:, j, c0:c0 + w], sml[:, 8:9], misc[:, 0:w], ALU.subtract, ALU.mult,
                        YB + [B("misc"), B("sml")], YB)
                    stt(ypre[:, j, c0:c0 + w], ypre[:, j, c0:c0 + w], sml[:, 10:11], misc[:, 1024:1024 + w], ALU.mult, ALU.add,
                        YB + [B("misc"), B("sml")], YB)
                didx = NPT if kind == "s" else idx
                dma(y_d[didx], ypre[:, j, :], reads=YB + [B("xT"), B("mixT")])
        tl.append(Task([], run_ln))
        return tl

    def chunks(lst, n):
        return [lst[i:i + n] for i in range(0, len(lst), n)]

    pre_groups = chunks([("x", i) for i in range(NPT)], GT)
    main_groups = chunks([("p", i) for i in range(NPT)] + [("s", 0)], GT)
    for gi, tiles in enumerate(pre_groups):
        tasks.extend(group_tasks(tiles, "pre", gi))

    def apply_flag(wslot, tick):
        drain()
        allc = [B(f"cst{h}") for h in range(NH)]
        ts("dve", cst[:].rearrange("p h c e -> p (h c e)"), cst[:].rearrange("p h c e -> p (h c e)"), flag[:, 0:1],
           ALU.mult, allc + [B("flag")], allc)
        ts("dve", gsm[:, 1:2], gsm[:, 1:2], flag[0:NH, 0:1], ALU.mult, [B("gsm"), B("flag")], [B("gsm")])
        em.op("dve", lambda e: e.memset(gsm[:, 0:1], 0.0), [], [B("gsm")])
    tasks.append(Task([], apply_flag))
    for gi, tiles in enumerate(main_groups):
        tasks.extend(group_tasks(tiles, "main", gi))

    def finale(wslot, tick):
        drain()
        allc = [B(f"cst{h}") for h in range(NH)]
        for h in range(NH):
            for dc in range(2):
                dma(pC_d[h, dc * 128:(dc + 1) * 128, :], cst[:, h, dc, 0:256], reads=[B(f"cst{h}")])
        cp("dve", hot[:, 0:2 * NH].rearrange("p (h c) -> p h c", c=2), cst[:, :, :, 256], allc, [B("hot")])
        tr(pbank[3][0:2 * NH, 0:128], hot[:, 0:2 * NH], ident_f[:], [B("hot")] + CONST, [PBK[3]])
        cp("dve", tmpm[0:2 * NH, 0:128], pbank[3][0:2 * NH, 0:128], [PBK[3]], [B("tmpm")])
        dma(pn_d.rearrange("h (c d) -> (h c) d", c=2), tmpm[0:2 * NH, 0:128], reads=[B("tmpm")])
        dma(pm_d, gsm[:, 1:2], reads=[B("gsm")])
        dma(om_d.rearrange("b h -> h b"), smo[:], reads=[B("smo")], allow_slow_non_contiguous=True)
        dma(ok_d[:, 0:120, :], ck_d[:, 8:128, :], reads=[B("ckshift")])
        dma(ov_d[:, 0:120, :], cv_d[:, 8:128, :], reads=[B("cvshift")])
    tasks.append(Task([], finale))

    def run_all():
        n = len(tasks)
        for i, t in enumerate(tasks):
            t.load_all()
            nxt = tasks[i + 1] if i + 1 < n else None
            if nxt is not None:
                nxt.load_all()

            nticks = max(1, getattr(t, "nticks", 3))
            hook["nxt"] = nxt
            nhooks = getattr(t, "nhooks", 0)
            quota = 0 if nxt is None else max(0, -(-(len(nxt.slabs) - nhooks) // nticks))

            def tick(nxt=nxt, quota=quota):
                if nxt is not None:
                    for _ in range(quota):
                        nxt.load_one()
                advance(2)
            t.run(t.wslot if t.wslot is not None else 0, tick)
    run_all()
    drain()
    em.finish()
    em.replay()
    st.close()
    return nc


def _consts(cfg, half):
    NPT, NH = cfg["NPT"], cfg["NH"]
    NT = NPT + 1
    TP = NPT * 128
    p = np.arange(128)
    t = p[:, None]
    j = np.arange(128)[None, :]
    neg = np.float32(NEG)
    z = np.float32(0)
    amask = np.full((128, 3, 256), neg, np.float32)
    amask[:, 0, 0:128] = np.where(j > t, z, neg)
    amask[:, 0, 128:256] = np.where(j <= t, z, neg)
    amask[:, 1] = amask[:, 0]
    if half == 0:
        amask[:, 1, 0:128] = neg
    tl = t % 8
    amask[:, 2, 0:128] = np.where(j > tl, z, neg)
    amask[:, 2, 128:256] = np.where((j // 8 == t // 8) & (j <= t), z, neg)
    dmask = np.full((128, 2, 128), neg, np.float32)
    dmask[:, 0, :] = np.where(t <= j, z, neg)
    dmask[:, 1, :] = np.where((t <= j) & (t // 8 == j // 8), z, neg)
    rope = np.zeros((128, NT + 1, 2, 64), np.float32)
    inv = (np.float32(ROPE_THETA) ** (-(np.arange(32, dtype=np.float32)) / np.float32(32))).astype(np.float32)
    for i in range(NT + 1):
        if i < NPT:
            pos = half * TP + i * 128 + p
        elif i == NPT:
            pos = PAST_LEN + p % 8
        else:
            pos = TP - 128 + p
        ang = (pos.astype(np.float32)[:, None] * inv[None, :]).astype(np.float32)
        c = np.cos(ang).astype(np.float32)
        s = np.sin(ang).astype(np.float32)
        rope[:, i, 0, 0:32] = c
        rope[:, i, 0, 32:64] = c
        rope[:, i, 1, 0:32] = -s
        rope[:, i, 1, 32:64] = s
    seqm = (p[:, None] // 8 == np.arange(16)[None, :]).astype(np.float32)
    colm = np.zeros((128, 16, 128), np.float32)
    colm[:, :, :] = (np.arange(128)[None, :] // 8 == np.arange(16)[:, None]).astype(np.float32)[None]
    rst = np.zeros((NH, 2, 128), np.float32)
    rst[:, 0, :] = np.where(np.arange(128) % 8 == 0, 0.0, 1.0)
    rst[:, 1, :] = np.where(np.arange(128) % 8 == 0, -1e30, 0.0)
    return dict(ident=np.eye(128, dtype=np.float32), amask=amask, dmask=dmask, rope=rope, seqm=seqm,
                colm=colm.reshape(128, 16 * 128).astype(ml_dtypes.bfloat16), rst=rst,
                flag=np.full((128, 1), float(half), np.float32))


def shard_inputs(cfg, inp, ncores):
    D, NPT, NH = cfg["D"], cfg["NPT"], cfg["NH"]
    TP = NPT * 128
    f = lambda a: np.ascontiguousarray(np.asarray(a, dtype=np.float32))
    rep = lambda v: np.ascontiguousarray(np.broadcast_to(f(v)[None, :], (128, v.shape[-1])))
    xpr, xs = f(inp["x_prompt"]), f(inp["x_sample"])
    KC = cfg["KC"]
    wt = np.zeros((128, cfg["INW"] * KC + 2 * D * (KC // 2) + D * KC), np.float32)
    mats = {k: f(inp[k][0]) for k in ("w_in", "w_bm", "w_ba", "w_out")}
    for (mat, c0, n, nkc), off in cfg["_pieces"].items():
        blk = mats[mat][0:nkc * 128, c0:c0 + n].reshape(nkc, 128, n).transpose(1, 0, 2)
        wt[:, off:off + nkc * n] = blk.reshape(128, nkc * n)
    shared = dict(wt=wt,
                  bif=rep(inp["b_if"][0]), sinks=rep(inp["attn_sinks"][0]), ng=rep(inp["norm_m_g"][0]),
                  lng=rep(inp["ln_g"][0]), lnb=rep(inp["ln_b"][0]))
    maps = []
    for c in range(ncores):
        b, half = c // 2, c % 2
        m = dict(shared)
        x = np.empty((NPT + 1, 128, D), np.float32)
        x[0:NPT] = xpr[b, half * TP:(half + 1) * TP].reshape(NPT, 128, D)
        x[NPT] = xs[c * 16:(c + 1) * 16].reshape(128, D)
        m["x"] = x
        m["xp"] = (xpr[b, 0:TP].reshape(NPT, 128, D).copy() if half == 1 else np.zeros((NPT, 128, D), np.float32))
        m["sC"] = f(inp["state_C"][0, c * 16:(c + 1) * 16])
        m["sn"] = f(inp["state_n"][0, c * 16:(c + 1) * 16])
        m["sm"] = f(inp["state_m"][0, c * 16:(c + 1) * 16])
        m["ck"] = f(inp["cache_k"][0, c * 16:(c + 1) * 16]).reshape(16, 128, NH * 64)
        m["cv"] = f(inp["cache_v"][0, c * 16:(c + 1) * 16]).reshape(16, 128, NH * 64)
        m.update(_consts(cfg, half))
        maps.append(m)
    return maps


def gather_outputs(cfg, res, ncores):
    D, NPT, NH = cfg["D"], cfg["NPT"], cfg["NH"]
    TP = NPT * 128
    nb = ncores // 2
    yp = np.empty((nb, 2 * TP, D), np.float32)
    ys = np.empty((ncores * 16, 8, D), np.float32)
    pC = np.empty((1, nb, NH, 256, 256), np.float32)
    pn = np.empty((1, nb, NH, 256), np.float32)
    pm = np.empty((1, nb, NH), np.float32)
    pk = np.empty((1, nb, 128, NH, 64), np.float32)
    pv = np.empty((1, nb, 128, NH, 64), np.float32)
    oC = np.empty((1, ncores * 16, NH, 256, 256), np.float32)
    on = np.empty((1, ncores * 16, NH, 256), np.float32)
    om = np.empty((1, ncores * 16, NH), np.float32)
    ok = np.empty((1, ncores * 16, 128, NH, 64), np.float32)
    ov = np.empty((1, ncores * 16, 128, NH, 64), np.float32)
    for c in range(ncores):
        r = res[c]
        b, half = c // 2, c % 2
        yp[b, half * TP:(half + 1) * TP] = r["y"][0:NPT].reshape(TP, D)
        ys[c * 16:(c + 1) * 16] = r["y"][NPT].reshape(16, 8, D)
        if half == 1:
            pC[0, b] = r["pC"]
            pn[0, b] = r["pn"]
            pm[0, b] = r["pm"][:, 0]
            pk[0, b] = r["pk"].reshape(128, NH, 64)
            pv[0, b] = r["pv"].reshape(128, NH, 64)
        sl = slice(c * 16, (c + 1) * 16)
        oC[0, sl] = r["oC"]
        on[0, sl] = r["on"]
        om[0, sl] = r["om"]
        ok[0, sl] = r["ok"].reshape(16, 128, NH, 64)
        ov[0, sl] = r["ov"].reshape(16, 128, NH, 64)
    return (yp, ys, pC, pn, pm, pk, pv, oC, on, om, ok, ov)


_NC_CACHE = {}


def kernel(**inputs):
    key = "full"
    if key not in _NC_CACHE:
        cfg = make_cfg(D=4096, NPT=8, GT=3)
        _NC_CACHE[key] = (build(cfg), cfg)
    nc, cfg = _NC_CACHE[key]
    maps = shard_inputs(cfg, inputs, 8)
    res = run_bass_kernel_spmd(nc, maps, core_ids=list(range(8)))
    return gather_outputs(cfg, res.results, 8)
```
